# Optimizing a Trainium2 kernel written in Bass

```python
import math
import jax, jax.numpy as jnp
from jax import lax
import numpy as np

D_MODEL = 1024
BATCH = 8
SEQ = 4096
DEPTH = 2

HEAD_DIM = 64
QBLK = 128
GRID_W = 64
EPS = 1e-6
A_HEADS = 8
A_KV = 2
A_WIN = 128
B_HEADS = 4
B_VDIM = 2 * HEAD_DIM
C_HEADS = 8
C_KV = 2
ROPE_THETA = 10000.0
D_PAIRS = ((128, 1), (512, 4), (2048, 16))
D_GROUPS = 3
D_HEADS_PER_GROUP = 8
D_BLK = 64
N_BRANCH = 4
BRANCH_W = 512
REL_BUCKETS = 32
REL_MAX_DIST = 1024
REL_HEADS = A_HEADS + B_HEADS + D_GROUPS * D_HEADS_PER_GROUP
IN_COLS = (
    ('a_q', A_HEADS * HEAD_DIM), ('a_k', A_KV * HEAD_DIM), ('a_v', A_KV * HEAD_DIM),
    ('b_q', B_HEADS * 2 * HEAD_DIM), ('b_k', B_HEADS * 2 * HEAD_DIM), ('b_v', B_HEADS * B_VDIM),
    ('c_q', C_HEADS * HEAD_DIM), ('c_k', C_KV * HEAD_DIM), ('c_v', C_KV * HEAD_DIM),
    ('d_q', D_GROUPS * D_HEADS_PER_GROUP * HEAD_DIM), ('d_k', D_GROUPS * D_HEADS_PER_GROUP * HEAD_DIM),
    ('d_v', D_GROUPS * D_HEADS_PER_GROUP * HEAD_DIM),
    ('gate', N_BRANCH * BRANCH_W),
    ('merge', N_BRANCH * D_MODEL),
)
IN_WIDTH = 13824

kernel_name = 'hybrid_gated_parallel_encoder'


def _rmsnorm(t, g):
    tf = t.astype(jnp.float32)
    y = tf * lax.rsqrt(jnp.mean(tf * tf, axis=-1, keepdims=True) + EPS)
    return (y * g.astype(jnp.float32)).astype(t.dtype)


def _project(h, w):
    out, off = {}, 0
    for name, width in IN_COLS:
        out[name] = h @ w[:, off:off + width]
        off += width
    return out


def _rel_bucket(rel):
    nb = REL_BUCKETS // 2
    exact = nb // 2
    n = jnp.abs(rel)
    large = exact + (jnp.log(jnp.maximum(n, exact).astype(jnp.float32) / exact)
                     / math.log(REL_MAX_DIST / exact) * (nb - exact)).astype(jnp.int32)
    large = jnp.minimum(large, nb - 1)
    return jnp.where(rel > 0, nb, 0) + jnp.where(n < exact, n, large)


def _band_rel(blk):
    return jnp.arange(3 * blk)[None, :] - blk - jnp.arange(blk)[:, None]


def _band(t, blk):
    bsz, L = t.shape[:2]
    nb = L // blk
    tp = jnp.pad(t, ((0, 0), (blk, blk)) + ((0, 0),) * (t.ndim - 2))
    tb = tp.reshape(bsz, nb + 2, blk, *t.shape[2:])
    return jnp.concatenate([tb[:, :-2], tb[:, 1:-1], tb[:, 2:]], axis=2)


def _band_logits(q, k, bias, blk, half_win, length):
    bsz, L = q.shape[:2]
    nb = L // blk
    qb = q.reshape(bsz, nb, blk, *q.shape[2:])
    s = jnp.einsum('bnqhgd,bnkhd->bnhgqk', qb, _band(k, blk)).astype(jnp.float32)
    s = s * (q.shape[-1] ** -0.5) + bias
    kidx = jnp.arange(nb)[:, None] * blk - blk + jnp.arange(3 * blk)[None, :]
    valid = (jnp.abs(_band_rel(blk)) <= half_win)[None] & ((kidx >= 0) & (kidx < length))[:, None, :]
    return jnp.where(valid[None, :, None, None], s, -jnp.inf)


def _band_values(p, v, blk):
    o = jnp.einsum('bnhgqk,bnkhd->bnqhgd', p, _band(v, blk))
    return o.reshape(o.shape[0], o.shape[1] * o.shape[2], *o.shape[3:])


def _window_sink_attention(q, k, v, bias, sink):
    s = _band_logits(q, k, bias, QBLK, A_WIN, q.shape[1])
    sk = sink.astype(jnp.float32)[:, :, None]
    m = jnp.maximum(jnp.max(s, axis=-1), sk)
    e = jnp.exp(s - m[..., None])
    den = jnp.sum(e, axis=-1) + jnp.exp(sk - m)
    o = _band_values((e / den[..., None]).astype(v.dtype), v, QBLK)
    return o.reshape(q.shape[0], q.shape[1], -1)


def _diff_attention(q, k, v, tbl, lam, lam_init, sub_gain):
    bsz, seq = q.shape[:2]
    nb = seq // QBLK
    scale = q.shape[-1] ** -0.5
    qb = jnp.moveaxis(q.reshape(bsz, nb, QBLK, *q.shape[2:]), 1, 0)
    kpos = jnp.arange(seq)

    def block(args):
        qi, i = args
        s = jnp.einsum('bqhcd,bkhcd->bhcqk', qi, k).astype(jnp.float32) * scale
        rel = kpos[None, :] - (i * QBLK + jnp.arange(QBLK))[:, None]
        bias = jnp.moveaxis(tbl[_rel_bucket(rel)], -1, 0)
        p = jax.nn.softmax(s + bias[None, :, None], axis=-1)
        w = (p[:, :, 0] - lam * p[:, :, 1]).astype(v.dtype)
        return jnp.einsum('bhqk,bkhe->bqhe', w, v)

    o = lax.map(block, (qb, jnp.arange(nb)))
    o = jnp.moveaxis(o, 0, 1).reshape(bsz, seq, *v.shape[2:])
    o = _rmsnorm(o, sub_gain) * (1.0 - lam_init)
    return o.reshape(bsz, seq, -1)


def _axial_rope(t, row, col):
    half = HEAD_DIM // 2
    nf = half // 2
    freqs = ROPE_THETA ** (-jnp.arange(nf, dtype=jnp.float32) / nf)
    ang = jnp.concatenate([row[:, None] * freqs, col[:, None] * freqs], axis=-1)
    shape = (1, t.shape[1]) + (1,) * (t.ndim - 3) + (half,)
    cos, sin = jnp.cos(ang).reshape(shape), jnp.sin(ang).reshape(shape)
    tf = t.astype(jnp.float32)
    t1, t2 = tf[..., :half], tf[..., half:]
    return jnp.concatenate([t1 * cos - t2 * sin, t2 * cos + t1 * sin], axis=-1).astype(t.dtype)


def _dense_gqa(q, k, v):
    bsz, seq = q.shape[:2]
    nb = seq // QBLK
    scale = q.shape[-1] ** -0.5
    qb = jnp.moveaxis(q.reshape(bsz, nb, QBLK, *q.shape[2:]), 1, 0)

    def block(qi):
        s = jnp.einsum('bqhgd,bkhd->bhgqk', qi, k).astype(jnp.float32) * scale
        p = jax.nn.softmax(s, axis=-1).astype(v.dtype)
        return jnp.einsum('bhgqk,bkhd->bqhgd', p, v)

    o = lax.map(block, qb)
    return jnp.moveaxis(o, 0, 1).reshape(bsz, seq, -1)


def _dilated_attention(q, k, v, bias_d):
    bsz, seq = q.shape[:2]
    hg, d = q.shape[3], q.shape[4]
    outs, lses = [], []
    for g, (win, r) in enumerate(D_PAIRS):
        half_m = win // (2 * r)
        M = seq // r
        Mp = -(-M // D_BLK) * D_BLK

        def strided(t):
            t = jnp.moveaxis(t[:, :, g].reshape(bsz, M, r, hg, d), 2, 1).reshape(bsz * r, M, hg, d)
            return jnp.pad(t, ((0, 0), (0, Mp - M), (0, 0), (0, 0)))

        qg, kg, vg = strided(q), strided(k), strided(v)
        s = _band_logits(qg[:, :, :, None], kg, bias_d[g], D_BLK, half_m, M)
        m = jnp.max(s, axis=-1)
        e = jnp.exp(s - m[..., None])
        den = jnp.sum(e, axis=-1)
        o = _band_values((e / den[..., None]).astype(v.dtype), vg, D_BLK)
        lse = jnp.moveaxis(m + jnp.log(den), -1, 2).reshape(bsz * r, Mp, hg)
        o = jnp.moveaxis(o[:, :M, :, 0].reshape(bsz, r, M, hg, d), 1, 2).reshape(bsz, seq, hg, d)
        lse = jnp.moveaxis(lse[:, :M].reshape(bsz, r, M, hg), 1, 2).reshape(bsz, seq, hg)
        outs.append(o)
        lses.append(lse)
    w = jax.nn.softmax(jnp.stack(lses, axis=0), axis=0)
    o = jnp.sum(w[..., None] * jnp.stack(outs, axis=0).astype(jnp.float32), axis=0)
    return o.astype(v.dtype).reshape(bsz, seq, -1)


def setup_inputs(seed: int = 0) -> dict:
    key = jax.random.key(seed)
    ks = jax.random.split(key, 10)
    nq = jax.random.normal
    return {
        'x': nq(ks[0], (BATCH, SEQ, D_MODEL), jnp.float32),
        'w_in': nq(ks[1], (DEPTH, D_MODEL, IN_WIDTH), jnp.float32) * D_MODEL ** -0.5,
        'w_branch': nq(ks[2], (DEPTH, N_BRANCH, BRANCH_W, D_MODEL), jnp.float32) * BRANCH_W ** -0.5,
        'w_out': nq(ks[3], (DEPTH, D_MODEL, D_MODEL), jnp.float32) * D_MODEL ** -0.5,
        'norm_gain': 1.0 + 0.05 * nq(ks[4], (DEPTH, D_MODEL), jnp.float32),
        'qk_gain': 1.0 + 0.05 * nq(ks[5], (DEPTH, N_BRANCH, 2, HEAD_DIM), jnp.float32),
        'sink': 0.5 * nq(ks[6], (DEPTH, A_HEADS), jnp.float32),
        'lambda_vec': 0.1 * nq(ks[7], (DEPTH, 4, HEAD_DIM), jnp.float32),
        'sub_norm_gain': 1.0 + 0.05 * nq(ks[8], (DEPTH, B_VDIM), jnp.float32),
        'rel_bias': 0.5 * nq(ks[9], (REL_BUCKETS, REL_HEADS), jnp.float32),
    }


def reference(x, w_in, w_branch, w_out, norm_gain, qk_gain, sink, lambda_vec, sub_norm_gain, rel_bias):
    bsz, seq, _ = x.shape
    rows = seq // GRID_W
    row = jnp.repeat(jnp.arange(rows), GRID_W).astype(jnp.float32)
    col = jnp.tile(jnp.arange(GRID_W), rows).astype(jnp.float32)
    ga = A_HEADS // A_KV
    gc = C_HEADS // C_KV
    rb = rel_bias.astype(jnp.float32)
    bias_a = jnp.moveaxis(rb[:, :A_HEADS][_rel_bucket(_band_rel(QBLK))], -1, 0).reshape(A_KV, ga, QBLK, 3 * QBLK)
    tbl_b = rb[:, A_HEADS:A_HEADS + B_HEADS]
    bias_d = []
    for g, (win, r) in enumerate(D_PAIRS):
        lo = A_HEADS + B_HEADS + g * D_HEADS_PER_GROUP
        tb = rb[:, lo:lo + D_HEADS_PER_GROUP][_rel_bucket(_band_rel(D_BLK) * r)]
        bias_d.append(jnp.moveaxis(tb, -1, 0)[:, None])

    for l in range(DEPTH):
        h = _rmsnorm(x, norm_gain[l])
        p = _project(h, w_in[l])
        qa = _rmsnorm(p['a_q'].reshape(bsz, seq, A_KV, ga, HEAD_DIM), qk_gain[l, 0, 0])
        ka = _rmsnorm(p['a_k'].reshape(bsz, seq, A_KV, HEAD_DIM), qk_gain[l, 0, 1])
        va = p['a_v'].reshape(bsz, seq, A_KV, HEAD_DIM)
        oa = _window_sink_attention(qa, ka, va, bias_a, sink[l].reshape(A_KV, ga))
        qb = _rmsnorm(p['b_q'].reshape(bsz, seq, B_HEADS, 2, HEAD_DIM), qk_gain[l, 1, 0])
        kb = _rmsnorm(p['b_k'].reshape(bsz, seq, B_HEADS, 2, HEAD_DIM), qk_gain[l, 1, 1])
        vb = p['b_v'].reshape(bsz, seq, B_HEADS, B_VDIM)
        lam_init = 0.8 - 0.6 * math.exp(-0.3 * l)
        lv = lambda_vec[l].astype(jnp.float32)
        lam = jnp.exp(jnp.sum(lv[0] * lv[1])) - jnp.exp(jnp.sum(lv[2] * lv[3])) + lam_init
        ob = _diff_attention(qb, kb, vb, tbl_b, lam, lam_init, sub_norm_gain[l])
        qc = _axial_rope(_rmsnorm(p['c_q'].reshape(bsz, seq, C_KV, gc, HEAD_DIM), qk_gain[l, 2, 0]), row, col)
        kc = _axial_rope(_rmsnorm(p['c_k'].reshape(bsz, seq, C_KV, HEAD_DIM), qk_gain[l, 2, 1]), row, col)
        vc = p['c_v'].reshape(bsz, seq, C_KV, HEAD_DIM)
        oc = _dense_gqa(qc, kc, vc)
        dshape = (bsz, seq, D_GROUPS, D_HEADS_PER_GROUP, HEAD_DIM)
        qd = _rmsnorm(p['d_q'].reshape(dshape), qk_gain[l, 3, 0])
        kd = _rmsnorm(p['d_k'].reshape(dshape), qk_gain[l, 3, 1])
        vd = p['d_v'].reshape(dshape)
        od = _dilated_attention(qd, kd, vd, bias_d)
        merged = jnp.zeros_like(x)
        for n, o in enumerate((oa, ob, oc, od)):
            gated = o * jax.nn.silu(p['gate'][..., n * BRANCH_W:(n + 1) * BRANCH_W])
            y = gated @ w_branch[l, n]
            merged = merged + jax.nn.sigmoid(p['merge'][..., n * D_MODEL:(n + 1) * D_MODEL]) * y
        x = x + merged @ w_out[l]
    return x
```

```python
import math
import os
import contextlib
import numpy as np
import concourse.bass as bass
import concourse.mybir as mybir
from concourse.bass_utils import run_bass_kernel_spmd

F32 = mybir.dt.float32
BF16 = mybir.dt.bfloat16
AF = mybir.ActivationFunctionType
ALU = mybir.AluOpType
AX = mybir.AxisListType

SEQ = 4096
DM = 1024
NT = SEQ // 128
EPS = 1e-6
NEG = -30000.0
COL = dict(a_q=0, a_k=512, a_v=640, b_q=768, b_k=1280, b_v=1792, c_q=2304, c_k=2816, c_v=2944,
           d_q=3072, d_k=4608, d_v=6144, gate=7680, merge=9728)
D_R = (1, 4, 16)
VEC_N = 2304
STRIP_W = 2176


class Buf:
    __slots__ = ("name", "last_w", "rd_ops", "rd_dma", "dsem", "dcount", "shared", "wr")

    def __init__(self, name, shared=False):
        self.name = name
        self.last_w = None
        self.rd_ops = {}
        self.rd_dma = {}
        self.dsem = None
        self.dcount = 0
        self.shared = shared
        self.wr = {}


class Op:
    __slots__ = ("eng", "fn", "deps", "kind", "signal", "sem", "val", "dticket")

    def __init__(self, eng, fn, kind):
        self.eng = eng
        self.fn = fn
        self.kind = kind
        self.deps = []
        self.signal = False
        self.sem = None
        self.val = 0
        self.dticket = None


class Sched:
    ENGS = ("sp", "act", "pool", "pe", "dve")

    def __init__(self, nc, stack):
        self.nc = nc
        self.stack = stack
        self.streams = {e: [] for e in self.ENGS}
        self.cur_sem = {}
        self.nsem = 0
        self.all_dma = {}
        self.pending = {}
        self.sem_objs = []
        self.dsem_bufs = []
        self.free_dsems = []
        self.heavy = []

    def new_sem(self, name):
        s = self.stack.enter_context(self.nc.semaphore(f"{name}_{self.nsem}"))
        self.nsem += 1
        self.sem_objs.append(s)
        return (len(self.sem_objs) - 1, s)

    def new_epoch(self):
        self.cur_sem = {}

    def _esem(self, eng):
        if eng not in self.cur_sem:
            self.cur_sem[eng] = self.new_sem("e" + eng)
        return self.cur_sem[eng]

    def _collect(self, op, reads, writes, sync=None):
        deps = []
        for b in reads:
            if b.shared:
                deps.extend(b.wr.values())
            elif b.last_w is not None:
                deps.append(b.last_w)
        for b in writes:
            if not b.shared and b.last_w is not None:
                lw = b.last_w
                same = (op.kind == "dma" and isinstance(lw, tuple) and sync is not None and sync.dsem is not None
                        and lw[0][0] == sync.dsem[0] and not b.rd_ops and not b.rd_dma)
                if not same:
                    deps.append(lw)
            deps.extend(b.rd_ops.values())
            deps.extend(b.rd_dma.values())
        pb = self.pending.pop(op.eng, None)
        if pb is not None:
            deps.append(pb)
        out = []
        seen = set()
        for d in deps:
            if d is op:
                continue
            if isinstance(d, Op):
                if d.eng == "pe" and op.eng == "pe" and op.kind == "c":
                    continue
                if id(d) in seen:
                    continue
                seen.add(id(d))
                d.signal = True
            out.append(d)
        op.deps = out

    def add(self, eng, fn, r=(), w=()):
        op = Op(eng, fn, "c")
        op.sem = self._esem(eng)
        self._collect(op, r, w)
        for b in w:
            b.last_w = op
            b.rd_ops = {}
            b.rd_dma = {}
        for b in r:
            if b not in w:
                b.rd_ops[eng] = op
        self.streams[eng].append(op)
        return op

    def pe(self, fn, r=(), w=()):
        return self.add("pe", fn, r, w)

    def act(self, fn, r=(), w=()):
        return self.add("act", fn, r, w)

    def dve(self, fn, r=(), w=()):
        return self.add("dve", fn, r, w)

    def pool(self, fn, r=(), w=()):
        return self.add("pool", fn, r, w)

    def dma(self, eng, out, in_, r, w, heavy=False, **kw):
        op = Op(eng, lambda e: e.dma_start(out=out, in_=in_, **kw), "dma")
        sync = w if not w.shared else [b for b in r if not b.shared][0]
        self._collect(op, r, [w], sync)
        if heavy:
            if len(self.heavy) >= 2:
                op.deps.append(self.heavy[-2])
        if sync.dsem is None:
            if self.free_dsems:
                sync.dsem, sync.dcount = self.free_dsems.pop()
            else:
                sync.dsem = self.new_sem("d" + sync.name)
            self.dsem_bufs.append(sync)
        sync.dcount += 16
        t = (sync.dsem, sync.dcount)
        op.dticket = t
        self.all_dma[sync.dsem[0]] = t
        w.rd_ops = {}
        w.rd_dma = {}
        if w.shared:
            w.wr[sync.dsem[0]] = t
        else:
            w.last_w = t
        for b in r:
            b.rd_dma[sync.dsem[0]] = t
        if heavy:
            self.heavy.append(t)
        self.streams[eng].append(op)
        return op

    def mark(self):
        return len(self.dsem_bufs)

    def recycle(self, mark):
        for b in self.dsem_bufs[mark:]:
            self.free_dsems.append((b.dsem, b.dcount))
            b.dsem = None
        del self.dsem_bufs[mark:]

    def barrier(self):
        op = Op("sp", None, "bar")
        op.sem = self._esem("sp")
        deps = []
        for e in self.ENGS:
            if e == "sp":
                continue
            st = [o for o in self.streams[e] if o.kind == "c"]
            if st:
                st[-1].signal = True
                deps.append(st[-1])
        deps.extend(self.all_dma.values())
        pb = self.pending.pop("sp", None)
        if pb is not None:
            deps.append(pb)
        op.deps = deps
        op.signal = True
        self.streams["sp"].append(op)
        for e in self.ENGS:
            if e != "sp":
                self.pending[e] = op
        return op

    def finish(self, bufs):
        op = Op("sp", None, "wait")
        op.deps = []
        for b in bufs:
            op.deps.extend(b.wr.values())
        self.streams["sp"].append(op)

    def finalize(self):
        for e in self.ENGS:
            cnt = {}
            for op in self.streams[e]:
                if op.kind in ("c", "bar") and op.signal:
                    k = op.sem[0]
                    cnt[k] = cnt.get(k, 0) + 1
                    op.val = cnt[k]

    def emit(self, eng, e):
        waited = {}
        for op in self.streams[eng]:
            for d in op.deps:
                if isinstance(d, Op):
                    sem, val = d.sem, d.val
                else:
                    sem, val = d
                if waited.get(sem[0], 0) >= val:
                    continue
                e.wait_ge(sem[1], val)
                waited[sem[0]] = val
            if op.kind == "c":
                ins = op.fn(e)
                if op.signal:
                    ins.then_inc(op.sem[1], 1)
            elif op.kind == "dma":
                ins = op.fn(e)
                ins.then_inc(op.dticket[0][1], 16)
            elif op.kind == "bar":
                e.sem_inc(op.sem[1], 1)


class Arena:
    def __init__(self, ap, nbytes, sched=None):
        self.sched = sched
        self.ap = ap
        self.nbytes = nbytes
        self.off = 0
        self.stack = []
        self.peak = 0

    def push(self):
        self.stack.append((self.off, self.sched.mark()))

    def pop(self):
        self.off, mk = self.stack.pop()
        self.sched.recycle(mk)

    def alloc(self, shape, dtype, parts=128):
        esz = 2 if dtype == BF16 else 4
        n = 1
        for s in shape:
            n *= s
        nb = (n * esz + 63) // 64 * 64
        assert self.off + nb <= self.nbytes, f"arena overflow {self.off}+{nb}"
        a = self.ap[0:parts, self.off // 4:(self.off + nb) // 4]
        self.off += nb
        self.peak = max(self.peak, self.off)
        if dtype != F32:
            a = a.bitcast(dtype)
        a = a[:, 0:n]
        if len(shape) == 2:
            a = a.rearrange("p (a b) -> p a b", a=shape[0])
        elif len(shape) == 3:
            a = a.rearrange("p (a b c) -> p a b c", a=shape[0], b=shape[1])
        return a


class Ring:
    def __init__(self, arena, name, n, shape, dtype, parts=128):
        self.items = [(arena.alloc(shape, dtype, parts), Buf(f"{name}{i}")) for i in range(n)]
        self.i = 0

    def next(self):
        it = self.items[self.i % len(self.items)]
        self.i += 1
        return it


def _rel_bucket_np(rel):
    rel = np.asarray(rel, dtype=np.int64)
    n = np.abs(rel)
    large = 8 + (np.log(np.maximum(n, 8).astype(np.float32) / np.float32(8)) / np.float32(math.log(128.0)) * 8).astype(np.int32)
    large = np.minimum(large, 15)
    return np.where(rel > 0, 16, 0) + np.where(n < 8, n, large)


def make_consts():
    c = {}
    c["c_ident"] = np.eye(128, dtype=np.float32)
    bo = np.zeros((128, 128), np.float32)
    bo[:64, :64] = 1.0
    bo[64:, 64:] = 1.0
    c["c_blockones"] = bo
    c["c_flip"] = np.ascontiguousarray(np.eye(128, dtype=np.float32)[::-1])
    oh = np.zeros((5, 32, VEC_N), np.float32)
    mk = np.zeros((5, VEC_N), np.float32)
    i = np.arange(511)
    dl = 255 - i
    oh[0][_rel_bucket_np(dl), i] = 1.0
    mk[0, :511] = np.where(np.abs(dl) <= 128, 0.0, NEG)
    for g, r in enumerate(D_R):
        i = np.arange(383)
        dl = 191 - i
        oh[1 + g][_rel_bucket_np(dl * r), i] = 1.0
        mk[1 + g, :383] = np.where(np.abs(dl) <= 64, 0.0, NEG)
    i = np.arange(2303)
    dl = 1151 - i
    oh[4][_rel_bucket_np(dl), i] = 1.0
    c["c_onehot"] = oh
    c["c_mask"] = mk
    nf = 16
    freqs = (np.float32(10000.0) ** (-np.arange(nf, dtype=np.float32) / np.float32(nf))).astype(np.float32)
    t = np.arange(SEQ)
    row = (t // 64).astype(np.float32)
    colp = (t % 64).astype(np.float32)
    ang = np.concatenate([row[:, None] * freqs[None, :], colp[:, None] * freqs[None, :]], axis=-1).astype(np.float32)
    cos = np.cos(ang).astype(np.float32).T
    sin = np.sin(ang).astype(np.float32).T
    c["c_cos"] = np.ascontiguousarray(np.tile(cos, (4, 1)))
    c["c_sin"] = np.ascontiguousarray(np.tile(sin, (4, 1)))
    sg = np.where((np.arange(128) % 64) < 32, -1.0, 1.0).astype(np.float32)
    c["c_sign"] = sg.reshape(128, 1)
    return c


def build_program(layers=(0, 1), phases="PABCDF", debug=False):
    nc = bass.Bass("TRN2", target_bir_lowering=False)
    okind = "ExternalOutput" if debug else "Internal"

    def din(name, shape):
        return nc.dram_tensor(name, list(shape), F32, kind="ExternalInput")

    x_t = din("x", [SEQ, DM])
    w_in_t = din("w_in", [2, DM, 13824])
    w_br_t = din("w_branch", [2, 4, 512, DM])
    w_out_t = din("w_out", [2, DM, DM])
    ng_t = din("norm_gain", [2, DM])
    qg_t = din("qk_gain", [2, 4, 2, 64])
    sink_t = din("sink", [2, 8])
    lv_t = din("lambda_vec", [2, 4, 64])
    sg_t = din("sub_norm_gain", [2, 128])
    rb_t = din("rel_bias", [32, 36])
    c_ident = din("c_ident", [128, 128])
    c_bones = din("c_blockones", [128, 128])
    c_flip = din("c_flip", [128, 128])
    c_oh = din("c_onehot", [5, 32, VEC_N])
    c_mask = din("c_mask", [5, VEC_N])
    c_cos = din("c_cos", [128, SEQ])
    c_sin = din("c_sin", [128, SEQ])
    c_sign = din("c_sign", [128, 1])

    out_t = nc.dram_tensor("out", [SEQ, DM], F32, kind="ExternalOutput")
    xmid_t = nc.dram_tensor("xmid", [SEQ, DM], F32, kind=okind)
    hT_t = nc.dram_tensor("hT_d", [DM, SEQ], BF16, kind=okind)
    qkA_t = nc.dram_tensor("qkA", [640, SEQ], BF16, kind=okind)
    qkB_t = nc.dram_tensor("qkB", [1024, SEQ], BF16, kind=okind)
    qkC_t = nc.dram_tensor("qkC", [640, SEQ], BF16, kind=okind)
    qkD_t = nc.dram_tensor("qkD", [3072, SEQ], BF16, kind=okind)
    vAC_t = nc.dram_tensor("vAC", [SEQ, 256], BF16, kind=okind)
    vB_t = nc.dram_tensor("vB", [SEQ, 512], BF16, kind=okind)
    vD_t = nc.dram_tensor("vD", [3, SEQ, 512], BF16, kind=okind)
    gated_t = nc.dram_tensor("gated", [SEQ, 2048], BF16, kind=okind)
    dacc_t = nc.dram_tensor("dacc", [3, SEQ, 520], F32, kind=okind)
    vec_t = nc.dram_tensor("vec_d", [36, VEC_N], F32, kind=okind)
    bt_t = nc.dram_tensor("btile", [36, 128, STRIP_W], F32, kind=okind)

    x_ap, xmid_ap, out_ap = x_t.ap(), xmid_t.ap(), out_t.ap()
    w_in, w_br, w_out = w_in_t.ap(), w_br_t.ap(), w_out_t.ap()
    hT_d, gated_d = hT_t.ap(), gated_t.ap()
    qk_d = dict(A=qkA_t.ap(), B=qkB_t.ap(), C=qkC_t.ap(), D=qkD_t.ap())
    vAC_d, vB_d, vD_d, dacc_d = vAC_t.ap(), vB_t.ap(), vD_t.ap(), dacc_t.ap()
    vec_d, bt_d = vec_t.ap(), bt_t.ap()

    ARENA_BYTES = 200 * 1024
    stack = contextlib.ExitStack()
    with stack:
        arena_t = stack.enter_context(nc.sbuf_tensor("arena", [128, ARENA_BYTES // 4], F32))
        banks = [stack.enter_context(nc.psum_tensor(f"bank{i}", [128, 512], F32)) for i in range(8)]
        S = Sched(nc, stack)
        ar = Arena(arena_t, ARENA_BYTES, S)
        PB = [Buf(f"ps{i}") for i in range(8)]

        def bk(i):
            return banks[i][:, :]

        B_x = Buf("x", True)
        B_xmid = Buf("xmid", True)
        B_out = Buf("out", True)
        B_hT = Buf("hTd", True)
        B_qk = {k: Buf("qk" + k, True) for k in "ABCD"}
        B_vAC, B_vB, B_vD = Buf("vAC", True), Buf("vB", True), Buf("vD", True)
        B_gated, B_dacc = Buf("gated", True), Buf("dacc", True)
        B_vec, B_bt = Buf("vecd", True), Buf("btd", True)
        B_const = Buf("constin", True)

        ident_bf = ar.alloc([128], BF16)
        bones_bf = ar.alloc([128], BF16)
        sign_sb = ar.alloc([1], F32)
        cfar_sb = ar.alloc([8], F32)
        Bk_const = Buf("kconst")
        S.dma("pool", ident_bf, c_ident.ap(), [B_const], Bk_const)
        Bk2 = Buf("kconst2")
        S.dma("pool", bones_bf, c_bones.ap(), [B_const], Bk2)
        Bk3 = Buf("kconst3")
        S.dma("sp", sign_sb, c_sign.ap(), [B_const], Bk3)
        B_cfar = Buf("cfar")

        def phase_setup():
            ar.push()
            rb_sb = ar.alloc([36], F32, parts=32)
            B_rb = Buf("rb")
            S.dma("sp", rb_sb, rb_t.ap(), [B_const], B_rb)
            flip_sb = ar.alloc([128], F32)
            B_flip = Buf("flip")
            S.dma("sp", flip_sb, c_flip.ap(), [B_const], B_flip)
            oh_ring = Ring(ar, "oh", 2, [VEC_N], F32, parts=32)
            mk_ring = Ring(ar, "mk", 2, [VEC_N], F32, parts=32)
            vs_ring = Ring(ar, "vs", 2, [VEC_N], F32, parts=32)
            fams = [(0, 0, 8, 511), (1, 12, 8, 383), (2, 20, 8, 383), (3, 28, 8, 383), (4, 8, 4, 2303)]
            pi = 0
            for seg, c0, nh, n in fams:
                oh_sb, B_oh = oh_ring.next()
                mk_sb, B_mk = mk_ring.next()
                vs_sb, B_vs = vs_ring.next()
                S.dma("sp", oh_sb[:, 0:n], c_oh.ap()[seg, :, 0:n], [B_const], B_oh)
                msrc = bass.AP(c_mask, seg * VEC_N, [[0, nh], [1, n]])
                S.dma("sp", mk_sb[0:nh, 0:n], msrc, [B_const], B_mk)
                for a in range(0, n, 512):
                    wdt = min(512, n - a)
                    pb = pi % 2
                    pi += 1
                    S.pe(lambda e, pb=pb, c0=c0, nh=nh, a=a, wdt=wdt, oh_sb=oh_sb: e.matmul(
                        bk(pb)[0:nh, 0:wdt], lhsT=rb_sb[:, c0:c0 + nh], rhs=oh_sb[:, a:a + wdt], start=True, stop=True),
                        r=[B_rb, B_oh], w=[PB[pb]])
                    S.dve(lambda e, pb=pb, nh=nh, a=a, wdt=wdt, vs_sb=vs_sb, mk_sb=mk_sb: e.tensor_tensor(
                        out=vs_sb[0:nh, a:a + wdt], in0=bk(pb)[0:nh, 0:wdt], in1=mk_sb[0:nh, a:a + wdt], op=ALU.add),
                        r=[PB[pb], B_mk], w=[B_vs])
                S.dma("sp", vec_d[c0:c0 + nh, 0:n], vs_sb[0:nh, 0:n], [B_vs], B_vec)
            hk_ring = Ring(ar, "hk", 2, [STRIP_W], F32)
            tz_ring = Ring(ar, "tz", 2, [STRIP_W], F32)
            for h36 in range(36):
                W = 384 if h36 < 8 else (STRIP_W if h36 < 12 else 256)
                hk_sb, B_hk = hk_ring.next()
                tz_sb, B_tz = tz_ring.next()
                src = bass.AP(vec_t, h36 * VEC_N, [[1, 128], [1, W]])
                S.dma("sp", hk_sb[:, 0:W], src, [B_vec], B_hk)
                for a in range(0, W, 512):
                    wdt = min(512, W - a)
                    pb = pi % 2
                    pi += 1
                    S.pe(lambda e, pb=pb, a=a, wdt=wdt, hk_sb=hk_sb: e.matmul(
                        bk(pb)[:, 0:wdt], lhsT=flip_sb, rhs=hk_sb[:, a:a + wdt], start=True, stop=True),
                        r=[B_flip, B_hk], w=[PB[pb]])
                    S.dve(lambda e, pb=pb, a=a, wdt=wdt, tz_sb=tz_sb: e.tensor_copy(
                        out=tz_sb[:, a:a + wdt], in_=bk(pb)[:, 0:wdt]), r=[PB[pb]], w=[B_tz])
                S.dma("sp", bt_d[h36, :, 0:W], tz_sb[:, 0:W], [B_tz], B_bt)
            for h in range(4):
                for j, idx in enumerate((0, 2302)):
                    src = bass.AP(vec_t, (8 + h) * VEC_N + idx, [[0, 128], [1, 1]])
                    S.dma("sp", cfar_sb[:, 2 * h + j:2 * h + j + 1], src, [B_vec], B_cfar)
            S.barrier()
            ar.pop()

        phase_setup()

        for l in layers:
            S.new_epoch()
            x_src, B_xs = (x_ap, B_x) if l == 0 else (xmid_ap, B_xmid)
            x_dst, B_xd = (xmid_ap, B_xmid) if (l == 0 and len(layers) > 1 and layers[-1] != 0) else (out_ap, B_out)
            lam_init = 0.8 - 0.6 * math.exp(-0.3 * l)
            wl = w_in[l]

            ar.push()
            lcraw = ar.alloc([10], F32)
            lc = ar.alloc([10], F32)
            esink = ar.alloc([8], F32)
            lvb = ar.alloc([256], F32)
            ltmp = ar.alloc([128], F32)
            lsc = ar.alloc([8], F32)
            subg = ar.alloc([128], F32)
            B_lcraw, B_lc, B_esink, B_lvb, B_ltmp, B_lsc, B_subg = (Buf(n) for n in ("lcraw", "lc", "esink", "lvb", "ltmp", "lsc", "subg"))
            qg = qg_t
            for half in range(2):
                src = bass.AP(qg, l * 512, [[1, 64], [64, 8]])
                S.dma("sp", lcraw[64 * half:64 * half + 64, 0:8], src, [B_const], B_lcraw, allow_slow_non_contiguous=True)
            for s in range(2):
                base = l * 512 + (2 * 2 + s) * 64
                for q4 in range(4):
                    so = 32 if q4 % 2 == 0 else 0
                    src = bass.AP(qg, base + so, [[1, 32], [1, 1]])
                    S.dma("sp", lcraw[32 * q4:32 * q4 + 32, 8 + s:9 + s], src, [B_const], B_lcraw)
            for m in range(4):
                S.dve(lambda e, m=m: e.tensor_scalar(out=lc[:, m:m + 1], in0=lcraw[:, 2 * m:2 * m + 1], scalar1=0.125, scalar2=None, op0=ALU.mult),
                      r=[B_lcraw], w=[B_lc])
                S.dve(lambda e, m=m: e.tensor_copy(out=lc[:, 4 + m:5 + m], in_=lcraw[:, 2 * m + 1:2 * m + 2]), r=[B_lcraw], w=[B_lc])
            S.dve(lambda e: e.tensor_scalar(out=lc[:, 8:9], in0=lcraw[:, 8:9], scalar1=sign_sb[:, 0:1], scalar2=0.125, op0=ALU.mult, op1=ALU.mult),
                  r=[B_lcraw, Bk3], w=[B_lc])
            S.dve(lambda e: e.tensor_scalar(out=lc[:, 9:10], in0=lcraw[:, 9:10], scalar1=sign_sb[:, 0:1], scalar2=None, op0=ALU.mult),
                  r=[B_lcraw, Bk3], w=[B_lc])
            S.dma("sp", esink, bass.AP(sink_t, l * 8, [[0, 128], [1, 8]]), [B_const], B_esink)
            S.act(lambda e: e.activation(out=esink, in_=esink, func=AF.Exp), r=[B_esink], w=[B_esink])
            S.dma("sp", lvb, bass.AP(lv_t, l * 256, [[0, 128], [1, 256]]), [B_const], B_lvb)
            S.dve(lambda e: e.tensor_tensor(out=ltmp[:, 0:64], in0=lvb[:, 0:64], in1=lvb[:, 64:128], op=ALU.mult), r=[B_lvb], w=[B_ltmp])
            S.dve(lambda e: e.tensor_tensor(out=ltmp[:, 64:128], in0=lvb[:, 128:192], in1=lvb[:, 192:256], op=ALU.mult), r=[B_lvb], w=[B_ltmp])
            S.dve(lambda e: e.tensor_reduce(out=lsc[:, 0:2], in_=ltmp.rearrange("p (a b) -> p a b", a=2), axis=AX.X, op=ALU.add), r=[B_ltmp], w=[B_lsc])
            S.act(lambda e: e.activation(out=lsc[:, 2:4], in_=lsc[:, 0:2], func=AF.Exp), r=[B_lsc], w=[B_lsc])
            S.dve(lambda e: e.tensor_tensor(out=lsc[:, 4:5], in0=lsc[:, 3:4], in1=lsc[:, 2:3], op=ALU.subtract), r=[B_lsc], w=[B_lsc])
            S.dve(lambda e, li=lam_init: e.tensor_scalar(out=lsc[:, 5:6], in0=lsc[:, 4:5], scalar1=-li, scalar2=None, op0=ALU.add), r=[B_lsc], w=[B_lsc])
            neglam = lsc[:, 5:6]
            S.dma("sp", subg, bass.AP(sg_t, l * 128, [[0, 128], [1, 128]]), [B_const], B_subg)
            S.dve(lambda e, li=lam_init: e.tensor_scalar(out=subg, in0=subg, scalar1=1.0 - li, scalar2=None, op0=ALU.mult), r=[B_subg], w=[B_subg])

            def phase_P():
                ar.push()
                hT_sb = ar.alloc([8, SEQ], BF16)
                B_hTsb = Buf("hTsb")
                gain_bc = ar.alloc([DM], F32)
                B_gain = Buf("gainbc")
                S.dma("sp", gain_bc, bass.AP(ng_t, l * DM, [[0, 128], [1, DM]]), [B_const], B_gain)
                ar.push()
                x_ring = Ring(ar, "xt", 2, [DM], F32)
                hb_ring = Ring(ar, "hb", 2, [DM], BF16)
                junk = ar.alloc([DM], BF16)
                B_junk = Buf("junk")
                st_ring = Ring(ar, "st", 2, [4], F32)
                for tt in range(NT):
                    xt, B_xt = x_ring.next()
                    hb, B_hb = hb_ring.next()
                    st, B_st = st_ring.next()
                    pb = tt % 2
                    S.dma("sp", xt, x_src[tt * 128:(tt + 1) * 128, :], [B_xs], B_xt)
                    S.act(lambda e, xt=xt, st=st: e.activation(out=junk, in_=xt, func=AF.Square, accum_out=st[:, 0:1]), r=[B_xt], w=[B_junk, B_st])
                    S.act(lambda e, st=st: e.activation(out=st[:, 1:2], in_=st[:, 0:1], func=AF.Ln, scale=1.0 / DM, bias=EPS), r=[B_st], w=[B_st])
                    S.act(lambda e, st=st: e.activation(out=st[:, 2:3], in_=st[:, 1:2], func=AF.Exp, scale=-0.5), r=[B_st], w=[B_st])
                    S.dve(lambda e, xt=xt, st=st, hb=hb: e.scalar_tensor_tensor(out=hb, in0=xt, scalar=st[:, 2:3], in1=gain_bc, op0=ALU.mult, op1=ALU.mult),
                          r=[B_xt, B_st, B_gain], w=[B_hb])
                    tp = bk(pb).bitcast(BF16)
                    for kc in range(8):
                        S.pe(lambda e, tp=tp, hb=hb, kc=kc: e.transpose(out=tp[:, kc * 128:(kc + 1) * 128], in_=hb[:, kc * 128:(kc + 1) * 128], identity=ident_bf),
                             r=[B_hb, Bk_const], w=[PB[pb]])
                    S.dve(lambda e, tp=tp, tt=tt: e.tensor_copy(out=hT_sb[:, :, tt * 128:(tt + 1) * 128], in_=tp.rearrange("p (a b) -> p a b", a=8)),
                          r=[PB[pb]], w=[B_hTsb])
                for kc in range(8):
                    S.dma("sp", hT_d[kc * 128:(kc + 1) * 128, :], hT_sb[:, kc, :], [B_hTsb], B_hT)
                S.barrier()
                ar.pop()

                ar.push()
                w_ring = Ring(ar, "wc", 3, [8, 128], BF16)
                sq_ring = Ring(ar, "sq", 4, [512], BF16)
                ln_ring = Ring(ar, "ln", 2, [512], F32)
                rs_ring = Ring(ar, "rs", 2, [512], F32)
                y_ring = Ring(ar, "y", 3, [512], BF16)
                vo_ring = Ring(ar, "vo", 4, [512], BF16)
                wvs = ar.alloc([8, 512], BF16)
                B_wvs = Buf("wvs")
                pstate = dict(mi=0, vi=0)

                def run_chunks(chunks, src, B_src, rope_res=None):
                    wres = []
                    for ch in chunks:
                        wres.append((w_ring.next(), rope_res["wp"].next() if ch["rope"] is not None else None))

                    def load_w(ci):
                        ch = chunks[ci]
                        (wc, B_wc), wpp = wres[ci]
                        c0 = ch["c0"]
                        S.dma("pool", wc, wl[:, c0:c0 + 128].rearrange("(kc p) c -> p kc c", p=128), [B_const], B_wc, heavy=True)
                        if wpp is not None:
                            wp, B_wp = wpp
                            for q4 in range(4):
                                so = (32 if q4 % 2 == 0 else -32) + 32 * q4
                                S.dma("pool", wp[:, :, 32 * q4:32 * q4 + 32], wl[:, c0 + so:c0 + so + 32].rearrange("(kc p) c -> p kc c", p=128), [B_const], B_wp, heavy=True)

                    def mk_ptile(ci, tc, mi):
                        ch = chunks[ci]
                        (wc, B_wc), wpp = wres[ci]
                        pq = mi % 4
                        pp = 4 + (mi % 3)
                        psn = 7
                        sq, B_sq = sq_ring.next()
                        lnv, B_ln = ln_ring.next()
                        rs, B_rs = rs_ring.next()
                        y, B_y = y_ring.next()
                        g = ch["g"]
                        if ch["rope"] is not None:
                            wp, B_wp = wpp
                            gp = ch["rope"]
                            y1, B_y1 = rope_res["y1"].next()
                            y2, B_y2 = rope_res["y2"].next()
                            cs, B_cs = rope_res["cs"].next()

                        def sa():
                            if tc == 0 and ci + 2 < len(chunks):
                                load_w(ci + 2)
                            for kc in range(8):
                                S.pe(lambda e, kc=kc: e.matmul(bk(pq), lhsT=wc[:, kc, :], rhs=src[:, kc, tc * 512:(tc + 1) * 512], start=(kc == 0), stop=(kc == 7)),
                                     r=[B_wc, B_src], w=[PB[pq]])
                            if ch["rope"] is not None:
                                for kc in range(8):
                                    S.pe(lambda e, kc=kc: e.matmul(bk(pp), lhsT=wp[:, kc, :], rhs=src[:, kc, tc * 512:(tc + 1) * 512], start=(kc == 0), stop=(kc == 7)),
                                         r=[B_wp, B_src], w=[PB[pp]])
                                S.dma("sp", cs[:, 0, :], c_cos.ap()[:, tc * 512:(tc + 1) * 512], [B_const], B_cs)
                                S.dma("sp", cs[:, 1, :], c_sin.ap()[:, tc * 512:(tc + 1) * 512], [B_const], B_cs)
                            S.act(lambda e: e.activation(out=sq, in_=bk(pq), func=AF.Square), r=[PB[pq]], w=[B_sq])

                        def sb_():
                            S.pe(lambda e: e.matmul(bk(psn), lhsT=bones_bf, rhs=sq, start=True, stop=True), r=[B_sq, Bk2], w=[PB[psn]])
                            S.act(lambda e: e.activation(out=lnv, in_=bk(psn), func=AF.Ln, scale=1.0 / 64, bias=EPS), r=[PB[psn]], w=[B_ln])
                            S.act(lambda e: e.activation(out=rs, in_=lnv, func=AF.Exp, scale=-0.5), r=[B_ln], w=[B_rs])
                            if ch["rope"] is None:
                                S.dve(lambda e: e.scalar_tensor_tensor(out=y, in0=bk(pq), scalar=lc[:, g:g + 1], in1=rs, op0=ALU.mult, op1=ALU.mult),
                                      r=[PB[pq], B_rs, B_lc], w=[B_y])
                            else:
                                S.dve(lambda e: e.scalar_tensor_tensor(out=y1, in0=bk(pq), scalar=lc[:, g:g + 1], in1=rs, op0=ALU.mult, op1=ALU.mult),
                                      r=[PB[pq], B_rs, B_lc], w=[B_y1])
                                S.dve(lambda e: e.scalar_tensor_tensor(out=y2, in0=bk(pp), scalar=lc[:, gp:gp + 1], in1=rs, op0=ALU.mult, op1=ALU.mult),
                                      r=[PB[pp], B_rs, B_lc], w=[B_y2])
                                S.pool(lambda e: e.tensor_tensor(out=y1, in0=y1, in1=cs[:, 0, :], op=ALU.mult), r=[B_y1, B_cs], w=[B_y1])
                                S.pool(lambda e: e.tensor_tensor(out=y2, in0=y2, in1=cs[:, 1, :], op=ALU.mult), r=[B_y2, B_cs], w=[B_y2])
                                S.dve(lambda e: e.tensor_tensor(out=y, in0=y1, in1=y2, op=ALU.add), r=[B_y1, B_y2], w=[B_y])
                            S.dma("pool", ch["dst"][ch["row"]:ch["row"] + 128, tc * 512:(tc + 1) * 512], y, [B_y], ch["B"])
                        return (sa, sb_)

                    ptiles = []
                    for ci in range(len(chunks)):
                        for tc in range(8):
                            ptiles.append(mk_ptile(ci, tc, pstate["mi"]))
                            pstate["mi"] += 1
                    for ci in range(min(2, len(chunks))):
                        load_w(ci)
                    n_pt = len(ptiles)
                    for i in range(n_pt + 2):
                        if i >= 2:
                            ptiles[i - 2][1]()
                        if i < n_pt:
                            ptiles[i][0]()

                def run_vjobs(jobs, src, B_src):
                    for tt in range(NT):
                        for (wap, B_w, n, dst, Bd) in jobs:
                            pb = pstate["vi"] % 4
                            pstate["vi"] += 1
                            vo, B_vo = vo_ring.next()
                            for kc in range(8):
                                S.pe(lambda e, pb=pb, n=n, kc=kc, wap=wap, tt=tt: e.matmul(bk(pb)[:, 0:n], lhsT=src[:, kc, tt * 128:(tt + 1) * 128], rhs=wap[:, kc, 0:n],
                                                                                     start=(kc == 0), stop=(kc == 7)), r=[B_src, B_w], w=[PB[pb]])
                            if pstate["vi"] % 2 == 0:
                                S.act(lambda e, vo=vo, pb=pb, n=n: e.copy(out=vo[:, 0:n], in_=bk(pb)[:, 0:n]), r=[PB[pb]], w=[B_vo])
                            else:
                                S.dve(lambda e, vo=vo, pb=pb, n=n: e.tensor_copy(out=vo[:, 0:n], in_=bk(pb)[:, 0:n]), r=[PB[pb]], w=[B_vo])
                            S.dma("pool", dst[tt * 128:(tt + 1) * 128, :], vo[:, 0:n], [B_vo], Bd)

                def load_wvs(g3):
                    c0 = COL["d_v"] + 512 * g3
                    S.dma("pool", wvs, wl[:, c0:c0 + 512].rearrange("(kc p) c -> p kc c", p=128), [B_const], B_wvs, heavy=True)

                chunks = []
                for i in range(5):
                    chunks.append(dict(c0=COL["a_q"] + 128 * i, dst=qk_d["A"], row=128 * i, g=0 if i < 4 else 4, r=1, rope=None, B=B_qk["A"]))
                for i in range(8):
                    chunks.append(dict(c0=COL["b_q"] + 128 * i, dst=qk_d["B"], row=128 * i, g=1 if i < 4 else 5, r=1, rope=None, B=B_qk["B"]))
                for i in range(5):
                    chunks.append(dict(c0=COL["c_q"] + 128 * i, dst=qk_d["C"], row=128 * i, g=2 if i < 4 else 6, r=1, rope=8 if i < 4 else 9, B=B_qk["C"]))
                for i in range(24):
                    grp = (i % 12) // 4
                    chunks.append(dict(c0=COL["d_q"] + 128 * i, dst=qk_d["D"], row=128 * i, g=3 if i < 12 else 7, r=D_R[grp], rope=None, B=B_qk["D"]))

                ar.push()
                rope_res = dict(wp=Ring(ar, "wp", 3, [8, 128], BF16), y1=Ring(ar, "y1", 2, [512], F32), y2=Ring(ar, "y2", 2, [512], F32),
                                cs=Ring(ar, "cs", 4, [2, 512], F32))
                wv0 = ar.alloc([8, 768], BF16)
                B_wv0 = Buf("wv0")
                for (c0, n, o) in [(COL["a_v"], 128, 0), (COL["c_v"], 128, 128), (COL["b_v"], 512, 256)]:
                    S.dma("pool", wv0[:, :, o:o + n], wl[:, c0:c0 + n].rearrange("(kc p) c -> p kc c", p=128), [B_const], B_wv0, heavy=True)
                load_wvs(0)
                run_chunks([c for c in chunks if c["r"] == 1], hT_sb, B_hTsb, rope_res)
                run_vjobs([(wv0[:, :, 0:256], B_wv0, 256, vAC_d, B_vAC), (wv0[:, :, 256:768], B_wv0, 512, vB_d, B_vB), (wvs, B_wvs, 512, vD_d[0], B_vD)], hT_sb, B_hTsb)
                S.barrier()
                ar.pop()
                ar.push()
                hT_p = ar.alloc([8, SEQ], BF16)
                B_hTp = Buf("hTp")
                for g3 in (1, 2):
                    r = D_R[g3]
                    for kc in range(8):
                        eng = S.dve if kc % 2 == 0 else S.pool
                        eng(lambda e, kc=kc, r=r: e.tensor_copy(out=hT_p[:, kc, :].rearrange("p (c m) -> p c m", c=r), in_=hT_sb[:, kc, :].rearrange("p (m c) -> p c m", c=r)),
                            r=[B_hTsb], w=[B_hTp])
                    load_wvs(g3)
                    run_chunks([c for c in chunks if c["r"] == r], hT_p, B_hTp)
                    run_vjobs([(wvs, B_wvs, 512, vD_d[g3], B_vD)], hT_p, B_hTp)
                S.barrier()
                ar.pop()
                ar.pop()
                ar.pop()

            class Epi:
                def __init__(self, n, width, pbank):
                    self.n, self.c0, self.width, self.pbank = n, 0, width, pbank
                    self.wg = ar.alloc([8, width], BF16)
                    self.B_wg = Buf("wg")
                    self.hq_ring = Ring(ar, "hq", 2, [8, 512], BF16)
                    self.sg_ring = Ring(ar, "sg", 2, [width], F32)
                    self.gt_ring = Ring(ar, "gt", 2, [4, width], BF16)

                def load(self, c0):
                    self.c0 = c0
                    gc0 = COL["gate"] + self.n * 512 + c0
                    S.dma("pool", self.wg, wl[:, gc0:gc0 + self.width].rearrange("(kc p) c -> p kc c", p=128), [B_const], self.B_wg, heavy=True)

                def start(self, qt):
                    hq, B_hq = self.hq_ring.next()
                    S.dma("sp", hq, hT_d[:, qt * 512:(qt + 1) * 512].rearrange("(kc p) t -> p kc t", p=128), [B_hT], B_hq)
                    gt, B_gt = self.gt_ring.next()
                    return dict(qt=qt, hq=hq, B_hq=B_hq, gt=gt, B_gt=B_gt, c0=self.c0)

                def mm(self, cx, qb):
                    pb, wdt, hq = self.pbank, self.width, cx["hq"]
                    for kc in range(8):
                        S.pe(lambda e, kc=kc: e.matmul(bk(pb)[:, 0:wdt], lhsT=hq[:, kc, qb * 128:(qb + 1) * 128], rhs=self.wg[:, kc, :],
                                                       start=(kc == 0), stop=(kc == 7)), r=[cx["B_hq"], self.B_wg], w=[PB[pb]])

                def fin(self, cx, qb, o_qb, B_o):
                    pb, wdt, gt = self.pbank, self.width, cx["gt"]
                    sg, B_sg = self.sg_ring.next()
                    S.act(lambda e: e.activation(out=sg, in_=bk(pb)[:, 0:wdt], func=AF.Exp, scale=-1.0), r=[PB[pb]], w=[B_sg])
                    S.dve(lambda e: e.tensor_scalar(out=sg, in0=sg, scalar1=1.0, scalar2=None, op0=ALU.add), r=[B_sg], w=[B_sg])
                    S.dve(lambda e: e.reciprocal(out=sg, in_=sg), r=[B_sg], w=[B_sg])
                    S.dve(lambda e: e.tensor_tensor(out=sg, in0=sg, in1=o_qb, op=ALU.mult), r=[B_sg, B_o], w=[B_sg])
                    S.dve(lambda e: e.tensor_tensor(out=gt[:, qb, :], in0=sg, in1=bk(pb)[:, 0:wdt], op=ALU.mult), r=[B_sg, PB[pb]], w=[cx["B_gt"]])
                    if qb == 3:
                        gc = self.n * 512 + cx["c0"]
                        qt = cx["qt"]
                        S.dma("pool", gated_d[qt * 512:(qt + 1) * 512, gc:gc + wdt].rearrange("(qb p) w -> p qb w", p=128), gt, [cx["B_gt"]], B_gated)

                def run(self, qt, o_ap, B_o):
                    hq, B_hq = self.hq_ring.next()
                    S.dma("sp", hq, hT_d[:, qt * 512:(qt + 1) * 512].rearrange("(kc p) t -> p kc t", p=128), [B_hT], B_hq)
                    gt, B_gt = self.gt_ring.next()
                    pb = self.pbank
                    wdt = self.width
                    for qb in range(4):
                        sg, B_sg = self.sg_ring.next()
                        for kc in range(8):
                            S.pe(lambda e, hq=hq, kc=kc, qb=qb: e.matmul(bk(pb)[:, 0:wdt], lhsT=hq[:, kc, qb * 128:(qb + 1) * 128], rhs=self.wg[:, kc, :],
                                                                         start=(kc == 0), stop=(kc == 7)), r=[B_hq, self.B_wg], w=[PB[pb]])
                        S.act(lambda e, sg=sg: e.activation(out=sg, in_=bk(pb)[:, 0:wdt], func=AF.Exp, scale=-1.0), r=[PB[pb]], w=[B_sg])
                        S.dve(lambda e, sg=sg: e.tensor_scalar(out=sg, in0=sg, scalar1=1.0, scalar2=None, op0=ALU.add), r=[B_sg], w=[B_sg])
                        S.dve(lambda e, sg=sg: e.reciprocal(out=sg, in_=sg), r=[B_sg], w=[B_sg])
                        S.dve(lambda e, sg=sg, qb=qb: e.tensor_tensor(out=sg, in0=sg, in1=o_ap[:, qb, :], op=ALU.mult), r=[B_sg, B_o], w=[B_sg])
                        S.dve(lambda e, sg=sg, gt=gt, qb=qb: e.tensor_tensor(out=gt[:, qb, :], in0=sg, in1=bk(pb)[:, 0:wdt], op=ALU.mult), r=[B_sg, PB[pb]], w=[B_gt])
                    gc = self.n * 512 + self.c0
                    S.dma("pool", gated_d[qt * 512:(qt + 1) * 512, gc:gc + wdt].rearrange("(qb p) w -> p qb w", p=128), gt, [B_gt], B_gated)

            def load_v(v_sb, B_v, src, dv):
                S.dma("sp", v_sb[:, :, 0:dv], src.rearrange("(j p) e -> p j e", p=128), [B_vAC, B_vB, B_vD], B_v)

            def run_pipeline(tiles, lag, dq=None, state=None):
                dq = dq if dq is not None else []
                state = state if state is not None else {}
                n = len(tiles)
                i = 0
                while i < n + lag or dq:
                    state["i"] = i
                    if i < n:
                        tiles[i][0]()
                    if lag <= i < n + lag:
                        tiles[i - lag][1]()
                    k = 0
                    while k < len(dq):
                        if dq[k][0] <= i:
                            dq.pop(k)[1]()
                        else:
                            k += 1
                    i += 1

            SBK = [0, 1, 7]
            LAG = 2

            def phase_B():
                ar.push()
                kT_ring = Ring(ar, "kT", 2, [2, SEQ], BF16)
                for (kT, B_kT) in kT_ring.items:
                    S.pool(lambda e, kT=kT: e.memset(kT[64:128, 0, :], 0.0), w=[B_kT])
                    S.pool(lambda e, kT=kT: e.memset(kT[0:64, 1, :], 0.0), w=[B_kT])
                v_ring = Ring(ar, "vb", 2, [32, 129], BF16)
                for (v_sb, B_v) in v_ring.items:
                    S.pool(lambda e, v_sb=v_sb: e.memset(v_sb[:, :, 128:129], 1.0), w=[B_v])
                strip_ring = Ring(ar, "strip", 2, [STRIP_W], F32)
                q_ring = Ring(ar, "qb", 2, [512], BF16)
                sb_ring = Ring(ar, "sbias", 3, [512], F32)
                p_ring = Ring(ar, "pT", 5, [512], BF16)
                oc_ring = Ring(ar, "oc", 2, [2, 4, 128], F32)
                o_ring = Ring(ar, "ob", 2, [4, 128], F32)
                sm2_ring = Ring(ar, "smb", 3, [8], F32)
                rd_ring = Ring(ar, "rdb", 3, [4], F32)
                raw_ring = Ring(ar, "rawb", 3, [4, 129], F32)
                dq = []
                state = {}
                junk = ar.alloc([128], BF16)
                B_junk = Buf("junkb")
                epi = Epi(1, 128, 6)
                hres = [(kT_ring.next(), v_ring.next(), strip_ring.next()) for h in range(4)]
                qres = {(h, qt): q_ring.next() for h in range(4) for qt in range(8)}
                ocres = {(h, qt): oc_ring.next() for h in range(4) for qt in range(8)}
                ores = {(h, qt): o_ring.next() for h in range(4) for qt in range(8)}

                def load_head(h):
                    (kT, B_kT), (v_sb, B_v), (strip, B_strip) = hres[h]
                    S.dma("sp", kT[0:64, 0, :], qk_d["B"][512 + h * 128:512 + h * 128 + 64, :], [B_qk["B"]], B_kT)
                    S.dma("sp", kT[64:128, 1, :], qk_d["B"][512 + h * 128 + 64:512 + h * 128 + 128, :], [B_qk["B"]], B_kT)
                    load_v(v_sb, B_v, vB_d[:, h * 128:(h + 1) * 128], 128)
                    S.dma("sp", strip, bt_d[8 + h, :, :], [B_bt], B_strip)

                def load_q(h, qt):
                    qs, B_q = qres[(h, qt)]
                    S.dma("sp", qs, qk_d["B"][h * 128:(h + 1) * 128, qt * 512:(qt + 1) * 512], [B_qk["B"]], B_q)

                def mk_tile(h, qt, c, kc, ps, pT, B_p, sb, B_sb):
                    (kT, B_kT), (v_sb, B_v), (strip, B_strip) = hres[h]
                    qs, B_q = qres[(h, qt)]
                    oc, B_oc = ocres[(h, qt)]
                    o_sb, B_o = ores[(h, qt)]

                    def sa():
                        if qt == 4 and c == 0 and kc == 0 and h < 3:
                            load_head(h + 1)
                        if c == 1 and kc == 0:
                            nq = (h, qt + 1) if qt < 7 else (h + 1, 0)
                            if nq[0] < 4:
                                load_q(*nq)
                        S.pe(lambda e: e.matmul(bk(ps), lhsT=kT[:, c, kc * 128:(kc + 1) * 128], rhs=qs, start=True, stop=True),
                             r=[B_kT, B_q], w=[PB[ps]])
                        d = kc * 128 - qt * 512
                        if -640 <= d <= 1024:
                            j0 = 1024 - d
                            S.dve(lambda e: e.tensor_tensor(out=sb, in0=bk(ps), in1=strip[:, j0:j0 + 512], op=ALU.add), r=[PB[ps], B_strip], w=[B_sb])
                            S.act(lambda e: e.activation(out=pT, in_=sb, func=AF.Exp), r=[B_sb], w=[B_p])
                        else:
                            ci = 2 * h + (0 if d > 0 else 1)
                            S.act(lambda e: e.activation(out=pT, in_=bk(ps), func=AF.Exp, bias=cfar_sb[:, ci:ci + 1]), r=[PB[ps], B_cfar], w=[B_p])

                    def sb_():
                        for qb in range(4):
                            S.pe(lambda e, qb=qb: e.matmul(bk(2 + qb)[:, 0:129], lhsT=pT[:, qb * 128:(qb + 1) * 128], rhs=v_sb[:, kc, :],
                                                           start=(kc == 0), stop=(kc == 31)), r=[B_p, B_v], w=[PB[2 + qb]])
                        if kc != 31:
                            return
                        raw, B_raw = raw_ring.next()
                        rd, B_rd = rd_ring.next()
                        for qb in range(4):
                            S.dve(lambda e, qb=qb: e.tensor_copy(out=raw[:, qb, :], in_=bk(2 + qb)[:, 0:129]), r=[PB[2 + qb]], w=[B_raw])
                        i0 = state["i"]

                        def norm():
                            S.dve(lambda e: e.reciprocal(out=rd, in_=raw[:, :, 128]), r=[B_raw], w=[B_rd])
                            for qb in range(4):
                                S.dve(lambda e, qb=qb: e.tensor_scalar(out=oc[:, c, qb, :], in0=raw[:, qb, 0:128], scalar1=rd[:, qb:qb + 1], scalar2=None, op0=ALU.mult),
                                      r=[B_raw, B_rd], w=[B_oc])
                        dq.append((i0 + 1, norm))
                        if c != 1:
                            return
                        sm, B_sm = sm2_ring.next()
                        cx = {}

                        def d2():
                            S.dve(lambda e: e.scalar_tensor_tensor(out=oc[:, 0], in0=oc[:, 1], scalar=neglam, in1=oc[:, 0], op0=ALU.mult, op1=ALU.add),
                                  r=[B_oc, B_lsc], w=[B_oc])
                            cx.update(epi.start(qt))

                        def d3():
                            for qb in range(4):
                                S.act(lambda e, qb=qb: e.activation(out=junk, in_=oc[:, 0, qb, :], func=AF.Square, accum_out=sm[:, qb:qb + 1]),
                                      r=[B_oc], w=[B_junk, B_sm])

                        def d4():
                            S.act(lambda e: e.activation(out=sm[:, 0:4], in_=sm[:, 0:4], func=AF.Ln, scale=1.0 / 128, bias=EPS), r=[B_sm], w=[B_sm])
                            S.act(lambda e: e.activation(out=sm[:, 4:8], in_=sm[:, 0:4], func=AF.Exp, scale=-0.5), r=[B_sm], w=[B_sm])

                        def d5():
                            for qb in range(4):
                                S.dve(lambda e, qb=qb: e.scalar_tensor_tensor(out=o_sb[:, qb, :], in0=oc[:, 0, qb, :], scalar=sm[:, 4 + qb:5 + qb], in1=subg,
                                                                              op0=ALU.mult, op1=ALU.mult), r=[B_oc, B_sm, B_subg], w=[B_o])
                            epi.mm(cx, 0)
                        dq.append((i0 + 2, d2))
                        dq.append((i0 + 3, d3))
                        dq.append((i0 + 4, d4))
                        dq.append((i0 + 5, d5))
                        for qb in range(4):
                            def ef(qb=qb):
                                epi.fin(cx, qb, o_sb[:, qb, :], B_o)
                                if qb < 3:
                                    epi.mm(cx, qb + 1)
                                elif qt == 7 and h < 3:
                                    epi.load((h + 1) * 128)
                            dq.append((i0 + 7 + 2 * qb, ef))
                    return (sa, sb_)

                load_head(0)
                load_q(0, 0)
                epi.load(0)
                tiles = []
                ti = 0
                for h in range(4):
                    for qt in range(8):
                        for c in range(2):
                            for kc in range(32):
                                ps = SBK[ti % 3]
                                ti += 1
                                pT, B_p = p_ring.next()
                                sb, B_sb = sb_ring.next()
                                tiles.append(mk_tile(h, qt, c, kc, ps, pT, B_p, sb, B_sb))
                run_pipeline(tiles, LAG, dq, state)
                S.barrier()
                ar.pop()

            def phase_C():
                ar.push()
                kT_ring = Ring(ar, "kTc", 2, [2, SEQ], BF16)
                for (kT, B_kT) in kT_ring.items:
                    S.pool(lambda e, kT=kT: e.memset(kT[64:128, 0, :], 0.0), w=[B_kT])
                    S.pool(lambda e, kT=kT: e.memset(kT[0:64, 1, :], 0.0), w=[B_kT])
                v_ring = Ring(ar, "vc", 2, [32, 65], BF16)
                for (v_sb, B_v) in v_ring.items:
                    S.pool(lambda e, v_sb=v_sb: e.memset(v_sb[:, :, 64:65], 1.0), w=[B_v])
                q_ring = Ring(ar, "qc", 2, [2, 512], BF16)
                p_ring = Ring(ar, "pTc", 5, [512], BF16)
                o_ring = Ring(ar, "oc_", 2, [4, 256], F32)
                rd_ring = Ring(ar, "rdc", 3, [4], F32)
                raw_ring = Ring(ar, "rawc", 3, [4, 65], F32)
                dq = []
                state = {}
                epi = Epi(2, 256, 6)
                gres = [(kT_ring.next(), v_ring.next()) for g in range(2)]
                qres = {(g, qt): q_ring.next() for g in range(2) for qt in range(8)}
                ores = {(g, qt): o_ring.next() for g in range(2) for qt in range(8)}

                def load_group(g):
                    (kT, B_kT), (v_sb, B_v) = gres[g]
                    S.dma("sp", kT[0:64, 0, :], qk_d["C"][512 + g * 64:512 + (g + 1) * 64, :], [B_qk["C"]], B_kT)
                    S.dma("sp", kT[64:128, 1, :], qk_d["C"][512 + g * 64:512 + (g + 1) * 64, :], [B_qk["C"]], B_kT)
                    load_v(v_sb, B_v, vAC_d[:, 128 + g * 64:128 + (g + 1) * 64], 64)

                def load_q(g, qt):
                    qs, B_q = qres[(g, qt)]
                    S.dma("sp", qs, qk_d["C"][g * 256:(g + 1) * 256, qt * 512:(qt + 1) * 512].rearrange("(jj p) t -> p jj t", jj=2), [B_qk["C"]], B_q)

                def mk_tile(g, qt, j, kc, ps, pT, B_p):
                    (kT, B_kT), (v_sb, B_v) = gres[g]
                    qs, B_q = qres[(g, qt)]
                    o_sb, B_o = ores[(g, qt)]

                    def sa():
                        if g == 0 and qt == 4 and j == 0 and kc == 0:
                            load_group(1)
                        if j == 2 and kc == 0:
                            nq = (g, qt + 1) if qt < 7 else (g + 1, 0)
                            if nq[0] < 2:
                                load_q(*nq)
                        S.pe(lambda e: e.matmul(bk(ps), lhsT=kT[:, j % 2, kc * 128:(kc + 1) * 128], rhs=qs[:, j // 2, :], start=True, stop=True), r=[B_kT, B_q], w=[PB[ps]])
                        S.act(lambda e: e.activation(out=pT, in_=bk(ps), func=AF.Exp), r=[PB[ps]], w=[B_p])

                    def sb_():
                        for qb in range(4):
                            S.pe(lambda e, qb=qb: e.matmul(bk(2 + qb)[:, 0:65], lhsT=pT[:, qb * 128:(qb + 1) * 128], rhs=v_sb[:, kc, :],
                                                           start=(kc == 0), stop=(kc == 31)), r=[B_p, B_v], w=[PB[2 + qb]])
                        if kc != 31:
                            return
                        raw, B_raw = raw_ring.next()
                        rd, B_rd = rd_ring.next()
                        for qb in range(4):
                            S.dve(lambda e, qb=qb: e.tensor_copy(out=raw[:, qb, :], in_=bk(2 + qb)[:, 0:65]), r=[PB[2 + qb]], w=[B_raw])
                        i0 = state["i"]

                        def norm():
                            S.dve(lambda e: e.reciprocal(out=rd, in_=raw[:, :, 64]), r=[B_raw], w=[B_rd])
                            for qb in range(4):
                                S.dve(lambda e, qb=qb: e.tensor_scalar(out=o_sb[:, qb, j * 64:(j + 1) * 64], in0=raw[:, qb, 0:64], scalar1=rd[:, qb:qb + 1],
                                                                      scalar2=None, op0=ALU.mult), r=[B_raw, B_rd], w=[B_o])
                        dq.append((i0 + 1, norm))
                        if j != 3:
                            return
                        cx = {}

                        def d2():
                            cx.update(epi.start(qt))

                        def d3():
                            epi.mm(cx, 0)
                        dq.append((i0 + 2, d2))
                        dq.append((i0 + 3, d3))
                        for qb in range(4):
                            def ef(qb=qb):
                                epi.fin(cx, qb, o_sb[:, qb, :], B_o)
                                if qb < 3:
                                    epi.mm(cx, qb + 1)
                                elif qt == 7 and g == 0:
                                    epi.load(256)
                            dq.append((i0 + 5 + 2 * qb, ef))
                    return (sa, sb_)

                load_group(0)
                load_q(0, 0)
                epi.load(0)
                tiles = []
                ti = 0
                for g in range(2):
                    for qt in range(8):
                        for j in range(4):
                            for kc in range(32):
                                ps = SBK[ti % 3]
                                ti += 1
                                pT, B_p = p_ring.next()
                                tiles.append(mk_tile(g, qt, j, kc, ps, pT, B_p))
                run_pipeline(tiles, LAG, dq, state)
                S.barrier()
                ar.pop()

            def phase_A():
                ar.push()
                kT_ring = Ring(ar, "kTa", 2, [2, SEQ], BF16)
                for (kT, B_kT) in kT_ring.items:
                    S.pool(lambda e, kT=kT: e.memset(kT[64:128, 0, :], 0.0), w=[B_kT])
                    S.pool(lambda e, kT=kT: e.memset(kT[0:64, 1, :], 0.0), w=[B_kT])
                v_ring = Ring(ar, "va", 2, [32, 65], BF16)
                for (v_sb, B_v) in v_ring.items:
                    S.pool(lambda e, v_sb=v_sb: e.memset(v_sb[:, :, 64:65], 1.0), w=[B_v])
                qa_ring = Ring(ar, "qa", 1, [2, SEQ], BF16)
                bA_ring = Ring(ar, "bA", 2, [4, 384], F32)
                sb_ring = Ring(ar, "sba", 3, [384], F32)
                p_ring = Ring(ar, "pTa", 5, [384], BF16)
                o_ring = Ring(ar, "oa", 2, [32, 256], F32)
                epi = Epi(0, 256, 6)
                dq = []
                state = {}
                sm_ring = Ring(ar, "sma", 4, [2], F32)
                gres = [(kT_ring.next(), v_ring.next(), qa_ring.next(), bA_ring.next(), o_ring.next()) for g in range(2)]

                def load_group(g, with_q=True):
                    (kT, B_kT), (v_sb, B_v), (qa, B_qa), (bA, B_bA), _ = gres[g]
                    S.dma("sp", kT[0:64, 0, :], qk_d["A"][512 + g * 64:512 + (g + 1) * 64, :], [B_qk["A"]], B_kT)
                    S.dma("sp", kT[64:128, 1, :], qk_d["A"][512 + g * 64:512 + (g + 1) * 64, :], [B_qk["A"]], B_kT)
                    load_v(v_sb, B_v, vAC_d[:, g * 64:(g + 1) * 64], 64)
                    S.dma("sp", bA, bt_d[g * 4:(g + 1) * 4, :, 0:384].rearrange("h p w -> p h w"), [B_bt], B_bA)

                def load_qa(g):
                    (qa, B_qa) = gres[g][2]
                    S.dma("sp", qa, qk_d["A"][g * 256:(g + 1) * 256, :].rearrange("(jj p) t -> p jj t", jj=2), [B_qk["A"]], B_qa)

                def mk_tile(g, j, kc, ps, pT, B_p, sb, B_sb):
                    (kT, B_kT), (v_sb, B_v), (qa, B_qa), (bA, B_bA), (o_sb, B_o) = gres[g]
                    hh = g * 4 + j
                    ulo, uhi = max(kc - 1, 0), min(kc + 1, 31)
                    w_ = 128 * (uhi - ulo + 1)
                    boff = (ulo - (kc - 1)) * 128

                    def sa():
                        if g == 0 and j == 2 and kc == 0:
                            load_group(1)
                        if g == 1 and j == 0 and kc == 0:
                            load_qa(1)
                        S.pe(lambda e: e.matmul(bk(ps)[:, 0:w_], lhsT=kT[:, j % 2, kc * 128:(kc + 1) * 128], rhs=qa[:, j // 2, ulo * 128:ulo * 128 + w_], start=True, stop=True),
                             r=[B_kT, B_qa], w=[PB[ps]])
                        S.dve(lambda e: e.tensor_tensor(out=sb[:, 0:w_], in0=bk(ps)[:, 0:w_], in1=bA[:, j, boff:boff + w_], op=ALU.add), r=[PB[ps], B_bA], w=[B_sb])
                        S.act(lambda e: e.activation(out=pT[:, 0:w_], in_=sb[:, 0:w_], func=AF.Exp), r=[B_sb], w=[B_p])

                    def sb_():
                        for u in range(ulo, uhi + 1):
                            ab = 2 + (u % 4)
                            S.pe(lambda e, ab=ab, u=u: e.matmul(bk(ab)[:, 0:65], lhsT=pT[:, (u - ulo) * 128:(u - ulo + 1) * 128], rhs=v_sb[:, kc, :],
                                                                start=(kc == max(u - 1, 0)), stop=(kc == min(u + 1, 31))), r=[B_p, B_v], w=[PB[ab]])
                            if kc == min(u + 1, 31):
                                sm, B_sm = sm_ring.next()
                                S.dve(lambda e, sm=sm, ab=ab: e.tensor_tensor(out=sm[:, 0:1], in0=bk(ab)[:, 64:65], in1=esink[:, hh:hh + 1], op=ALU.add),
                                      r=[PB[ab], B_esink], w=[B_sm])
                                S.dve(lambda e, sm=sm: e.reciprocal(out=sm[:, 1:2], in_=sm[:, 0:1]), r=[B_sm], w=[B_sm])
                                S.act(lambda e, sm=sm, ab=ab, u=u: e.activation(out=o_sb[:, u, j * 64:(j + 1) * 64], in_=bk(ab)[:, 0:64], func=AF.Copy, scale=sm[:, 1:2]),
                                      r=[PB[ab], B_sm], w=[B_o])
                        if j == 3 and kc == 31:
                            i0 = state["i"]
                            for qt in range(8):
                                def er(qt=qt):
                                    epi.run(qt, o_sb[:, qt * 4:(qt + 1) * 4, :], B_o)
                                    if qt == 7 and g == 0:
                                        epi.load(256)
                                dq.append((i0 + 2 + 3 * qt, er))
                    return (sa, sb_)

                load_group(0)
                load_qa(0)
                epi.load(0)
                tiles = []
                ti = 0
                for g in range(2):
                    for j in range(4):
                        for kc in range(32):
                            ps = SBK[ti % 3]
                            ti += 1
                            pT, B_p = p_ring.next()
                            sb, B_sb = sb_ring.next()
                            tiles.append(mk_tile(g, j, kc, ps, pT, B_p, sb, B_sb))
                run_pipeline(tiles, LAG, dq, state)
                S.barrier()
                ar.pop()

            def phase_D():
                ar.push()
                HB = {1: 2, 4: 8, 16: 8}
                bD_ring = Ring(ar, "bD", 2, [8, 256], F32)
                sb_ring = Ring(ar, "sbd", 3, [256], F32)
                p_ring = Ring(ar, "pTd", 5, [256], BF16)
                stg_ring = Ring(ar, "stg", 3, [33, 65], F32)
                ti = 0
                ai = 0
                for g3 in range(3):
                    r = D_R[g3]
                    M = SEQ // r
                    nj = M // 128
                    hb_n = HB[r]
                    bD, B_bD = bD_ring.next()
                    S.dma("sp", bD, bt_d[12 + 8 * g3:20 + 8 * g3, :, 0:256].rearrange("h p w -> p h w"), [B_bt], B_bD)
                    ar.push()
                    kT_ring = Ring(ar, "kTd", 2, [hb_n, M], BF16)
                    q_ring = Ring(ar, "qd", 2, [hb_n // 2, M + 128], BF16)
                    v_ring = Ring(ar, "vd", 2, [hb_n, nj, 65], BF16)
                    for (v_sb, B_v) in v_ring.items:
                        S.pool(lambda e, v_sb=v_sb: e.memset(v_sb[:, :, :, 64:65], 1.0), w=[B_v])
                    for (q_sb, B_q) in q_ring.items:
                        S.pool(lambda e, q_sb=q_sb: e.memset(q_sb[:, :, 0:64], 0.0), w=[B_q])
                        S.pool(lambda e, q_sb=q_sb, M=M: e.memset(q_sb[:, :, 64 + M:128 + M], 0.0), w=[B_q])
                    for (kT, B_kT) in kT_ring.items:
                        S.pool(lambda e, kT=kT: e.memset(kT[64:128, 0::2, :], 0.0), w=[B_kT])
                        S.pool(lambda e, kT=kT: e.memset(kT[0:64, 1::2, :], 0.0), w=[B_kT])
                    batches = [(cc, h0) for cc in range(r) for h0 in range(0, 8, hb_n)]
                    bres = [(kT_ring.next(), q_ring.next(), v_ring.next()) for _ in batches]

                    def load_batch(bi, g3=g3, M=M, hb_n=hb_n, batches=batches, bres=bres):
                        cc, h0 = batches[bi]
                        (kT, B_kT), (q_sb, B_q), (v_sb, B_v) = bres[bi]
                        krow = 1536 + g3 * 512 + h0 * 64
                        qrow = g3 * 512 + h0 * 64
                        ksrc = qk_d["D"][krow:krow + hb_n * 64, cc * M:(cc + 1) * M].rearrange("(hp two d) t -> two d hp t", two=2, d=64)
                        S.dma("sp", kT[0:64, 0::2, :], ksrc[0], [B_qk["D"]], B_kT)
                        S.dma("sp", kT[64:128, 1::2, :], ksrc[1], [B_qk["D"]], B_kT)
                        S.dma("sp", q_sb[:, :, 64:64 + M], qk_d["D"][qrow:qrow + hb_n * 64, cc * M:(cc + 1) * M].rearrange("(hp p) t -> p hp t", p=128), [B_qk["D"]], B_q)
                        for hi in range(hb_n):
                            S.dma("sp", v_sb[:, hi, :, 0:64], vD_d[g3, cc * M:(cc + 1) * M, (h0 + hi) * 64:(h0 + hi + 1) * 64].rearrange("(j p) e -> p j e", p=128), [B_vD], B_v)

                    def mk_tile(bi, hi, kc, ps, pT, B_p, sb, B_sb, stg, B_stg, a_prev, a_new, g3=g3, r=r, M=M, nj=nj, hb_n=hb_n, batches=batches, bres=bres, bD=bD, B_bD=B_bD):
                        cc, h0 = batches[bi]
                        (kT, B_kT), (q_sb, B_q), (v_sb, B_v) = bres[bi]
                        hs = h0 + hi

                        def sa():
                            if hi * nj + kc == LAG and bi + 1 < len(batches):
                                load_batch(bi + 1)
                            S.pe(lambda e: e.matmul(bk(ps)[:, 0:256], lhsT=kT[:, hi, kc * 128:(kc + 1) * 128], rhs=q_sb[:, hi // 2, kc * 128:kc * 128 + 256], start=True, stop=True),
                                 r=[B_kT, B_q], w=[PB[ps]])
                            S.dve(lambda e: e.tensor_tensor(out=sb, in0=bk(ps)[:, 0:256], in1=bD[:, hs, :], op=ALU.add), r=[PB[ps], B_bD], w=[B_sb])
                            S.act(lambda e: e.activation(out=pT, in_=sb, func=AF.Exp), r=[B_sb], w=[B_p])

                        def sb_():
                            S.pe(lambda e: e.matmul(bk(a_prev)[:, 0:65], lhsT=pT[:, 0:128], rhs=v_sb[:, hi, kc, :], start=(kc == 0), stop=True), r=[B_p, B_v], w=[PB[a_prev]])
                            S.act(lambda e: e.copy(out=stg[:, kc, :], in_=bk(a_prev)[:, 0:65]), r=[PB[a_prev]], w=[B_stg])
                            S.pe(lambda e: e.matmul(bk(a_new)[:, 0:65], lhsT=pT[:, 128:256], rhs=v_sb[:, hi, kc, :], start=True, stop=(kc == nj - 1)), r=[B_p, B_v], w=[PB[a_new]])
                            if kc != nj - 1:
                                return
                            S.act(lambda e: e.copy(out=stg[:, nj, :], in_=bk(a_new)[:, 0:65]), r=[PB[a_new]], w=[B_stg])
                            base = g3 * SEQ * 520 + hs * 65
                            dst = bass.AP(dacc_t, base + cc * 520, [[r * 520, 64], [1, 65]])
                            S.dma("pool", dst, stg[64:128, 0, :], [B_stg], B_dacc)
                            if nj > 1:
                                dst = bass.AP(dacc_t, base + (cc + r * 64) * 520, [[r * 520, 128], [128 * r * 520, nj - 1], [1, 65]])
                                S.dma("pool", dst, stg[:, 1:nj, :], [B_stg], B_dacc)
                            dst = bass.AP(dacc_t, base + (cc + r * (M - 64)) * 520, [[r * 520, 64], [1, 65]])
                            S.dma("pool", dst, stg[0:64, nj, :], [B_stg], B_dacc)
                        return (sa, sb_)

                    load_batch(0)
                    tiles = []
                    for bi in range(len(batches)):
                        for hi in range(hb_n):
                            stg, B_stg = stg_ring.next()
                            cur = None
                            for kc in range(nj):
                                ps = SBK[ti % 3]
                                ti += 1
                                pT, B_p = p_ring.next()
                                sb, B_sb = sb_ring.next()
                                if kc == 0:
                                    cur = 2 + (ai % 4)
                                    ai += 1
                                a_prev = cur
                                cur = 2 + (ai % 4)
                                ai += 1
                                a_new = cur
                                tiles.append(mk_tile(bi, hi, kc, ps, pT, B_p, sb, B_sb, stg, B_stg, a_prev, a_new))
                    run_pipeline(tiles, LAG)
                    S.barrier()
                    ar.pop()
                ar.push()
                da_ring = Ring(ar, "da", 2, [3, 8, 65], F32)
                o_ring = Ring(ar, "od", 2, [4, 512], F32)
                sm_ring = Ring(ar, "smd", 2, [8], F32)
                epi = Epi(3, 512, 6)
                epi.load(0)
                for qt in range(8):
                    o_sb, B_o = o_ring.next()
                    for qb in range(4):
                        tt = qt * 4 + qb
                        da, B_da = da_ring.next()
                        sm, B_sm = sm_ring.next()
                        S.dma("sp", da, dacc_d[:, tt * 128:(tt + 1) * 128, :].rearrange("g p (h e) -> p g h e", h=8), [B_dacc], B_da)
                        S.dve(lambda e, da=da: e.tensor_tensor(out=da[:, 0], in0=da[:, 0], in1=da[:, 1], op=ALU.add), r=[B_da], w=[B_da])
                        S.dve(lambda e, da=da: e.tensor_tensor(out=da[:, 0], in0=da[:, 0], in1=da[:, 2], op=ALU.add), r=[B_da], w=[B_da])
                        S.dve(lambda e, da=da, sm=sm: e.reciprocal(out=sm, in_=da[:, 0, :, 64]), r=[B_da], w=[B_sm])
                        for hs in range(8):
                            S.dve(lambda e, da=da, sm=sm, hs=hs, o_sb=o_sb, qb=qb: e.tensor_scalar(out=o_sb[:, qb, hs * 64:(hs + 1) * 64], in0=da[:, 0, hs, 0:64],
                                                                                                 scalar1=sm[:, hs:hs + 1], scalar2=None, op0=ALU.mult), r=[B_da, B_sm], w=[B_o])
                    epi.run(qt, o_sb, B_o)
                S.barrier()
                ar.pop()
                ar.pop()

            def phase_F():
                ar.push()
                wbr = ar.alloc([4, 4, DM], BF16)
                wmg = ar.alloc([8, 4096], BF16)
                wo = ar.alloc([8, DM], BF16)
                B_wbr, B_wmg, B_wo = Buf("wbr"), Buf("wmg"), Buf("wo")
                for n in range(4):
                    S.dma("pool", wbr[:, n], w_br[l, n].rearrange("(kc p) o -> p kc o", p=128), [B_const], B_wbr, heavy=True)
                for n in range(4):
                    S.dma("pool", wmg[:, :, n * 1024:(n + 1) * 1024], wl[:, COL["merge"] + n * 1024:COL["merge"] + (n + 1) * 1024].rearrange("(kc p) c -> p kc c", p=128), [B_const], B_wmg, heavy=True)
                S.dma("pool", wo, w_out[l].rearrange("(kc p) o -> p kc o", p=128), [B_const], B_wo, heavy=True)
                g_ring = Ring(ar, "gin", 2, [2048], BF16)
                gT = ar.alloc([16, 512], BF16)
                B_gT = Buf("gT")
                hq_ring = Ring(ar, "hqf", 1, [8, 512], BF16)
                mT = ar.alloc([8, 512], F32)
                B_mT = Buf("mT")
                mTb = ar.alloc([8, 512], BF16)
                B_mTb = Buf("mTb")
                sg_ring = Ring(ar, "sgf", 2, [512], F32)
                xq_ring = Ring(ar, "xq", 2, [DM], F32)
                ti = 0
                for qt in range(8):
                    hq, B_hq = hq_ring.next()
                    S.dma("sp", hq, hT_d[:, qt * 512:(qt + 1) * 512].rearrange("(kc p) t -> p kc t", p=128), [B_hT], B_hq)
                    for qb in range(4):
                        gin, B_gin = g_ring.next()
                        tt = qt * 4 + qb
                        S.dma("sp", gin, gated_d[tt * 128:(tt + 1) * 128, :], [B_gated], B_gin)
                        for half in range(2):
                            pb = ti % 2
                            ti += 1
                            tp = bk(pb).bitcast(BF16)
                            for k8 in range(8):
                                kcg = half * 8 + k8
                                S.pe(lambda e, tp=tp, gin=gin, k8=k8, kcg=kcg: e.transpose(out=tp[:, k8 * 128:(k8 + 1) * 128], in_=gin[:, kcg * 128:(kcg + 1) * 128], identity=ident_bf),
                                     r=[B_gin, Bk_const], w=[PB[pb]])
                            S.dve(lambda e, tp=tp, half=half, qb=qb: e.tensor_copy(out=gT[:, half * 8:(half + 1) * 8, qb * 128:(qb + 1) * 128], in_=tp.rearrange("p (a b) -> p a b", a=8)),
                                  r=[PB[pb]], w=[B_gT])
                    yi = 0
                    for n in range(4):
                        for oc in range(8):
                            py = 2 + (yi % 2)
                            pm = 4 + (yi % 2)
                            yi += 1
                            for kc in range(4):
                                S.pe(lambda e, py=py, n=n, kc=kc, oc=oc: e.matmul(bk(py), lhsT=wbr[:, n, kc, oc * 128:(oc + 1) * 128], rhs=gT[:, n * 4 + kc, :], start=(kc == 0), stop=(kc == 3)),
                                     r=[B_wbr, B_gT], w=[PB[py]])
                            for kc in range(8):
                                S.pe(lambda e, pm=pm, n=n, kc=kc, oc=oc, hq=hq: e.matmul(bk(pm), lhsT=wmg[:, kc, n * 1024 + oc * 128:n * 1024 + (oc + 1) * 128], rhs=hq[:, kc, :],
                                                                                      start=(kc == 0), stop=(kc == 7)), r=[B_wmg, B_hq], w=[PB[pm]])
                            sg, B_sg = sg_ring.next()
                            S.act(lambda e, sg=sg, pm=pm: e.activation(out=sg, in_=bk(pm), func=AF.Sigmoid), r=[PB[pm]], w=[B_sg])
                            if n == 0:
                                S.dve(lambda e, sg=sg, py=py, oc=oc: e.tensor_tensor(out=mT[:, oc, :], in0=sg, in1=bk(py), op=ALU.mult), r=[B_sg, PB[py]], w=[B_mT])
                            else:
                                S.dve(lambda e, sg=sg, py=py: e.tensor_tensor(out=sg, in0=sg, in1=bk(py), op=ALU.mult), r=[B_sg, PB[py]], w=[B_sg])
                                if n < 3:
                                    S.pool(lambda e, sg=sg, oc=oc: e.tensor_tensor(out=mT[:, oc, :], in0=mT[:, oc, :], in1=sg, op=ALU.add), r=[B_sg, B_mT], w=[B_mT])
                                else:
                                    S.pool(lambda e, sg=sg, oc=oc: e.tensor_tensor(out=mTb[:, oc, :], in0=mT[:, oc, :], in1=sg, op=ALU.add), r=[B_sg, B_mT], w=[B_mTb])
                    for qb in range(4):
                        tt = qt * 4 + qb
                        xq, B_xq = xq_ring.next()
                        S.dma("sp", xq, x_src[tt * 128:(tt + 1) * 128, :], [B_xs], B_xq)
                        for half in range(2):
                            po = 6 + (half % 2)
                            for kc in range(8):
                                S.pe(lambda e, po=po, kc=kc, qb=qb, half=half: e.matmul(bk(po), lhsT=mTb[:, kc, qb * 128:(qb + 1) * 128], rhs=wo[:, kc, half * 512:(half + 1) * 512],
                                                                                     start=(kc == 0), stop=(kc == 7)), r=[B_mTb, B_wo], w=[PB[po]])
                            S.dve(lambda e, po=po, xq=xq, half=half: e.tensor_tensor(out=xq[:, half * 512:(half + 1) * 512], in0=xq[:, half * 512:(half + 1) * 512], in1=bk(po), op=ALU.add),
                                  r=[PB[po], B_xq], w=[B_xq])
                        S.dma("pool", x_dst[tt * 128:(tt + 1) * 128, :], xq, [B_xq], B_xd)
                S.barrier()
                ar.pop()

            if "P" in phases:
                phase_P()
            if "A" in phases:
                phase_A()
            if "B" in phases:
                phase_B()
            if "C" in phases:
                phase_C()
            if "D" in phases:
                phase_D()
            if "F" in phases:
                phase_F()
            ar.pop()

        S.finish([B_out, B_xmid, B_hT, B_gated, B_dacc, B_bt, B_vec, B_vAC, B_vB, B_vD] + list(B_qk.values()))
        S.finalize()
        with nc.Block() as block:
            @block.sync
            def _(e):
                S.emit("sp", e)

            @block.scalar
            def _(e):
                S.emit("act", e)

            @block.gpsimd
            def _(e):
                S.emit("pool", e)

            @block.tensor
            def _(e):
                S.emit("pe", e)

            @block.vector
            def _(e):
                S.emit("dve", e)
    return nc


_CONSTS = None


def kernel(x, w_in, w_branch, w_out, norm_gain, qk_gain, sink, lambda_vec, sub_norm_gain, rel_bias):
    global _CONSTS
    if _CONSTS is None:
        _CONSTS = make_consts()
    f = lambda a: np.ascontiguousarray(np.asarray(a, dtype=np.float32))
    shared = dict(w_in=f(w_in), w_branch=f(w_branch), w_out=f(w_out), norm_gain=f(norm_gain), qk_gain=f(qk_gain), sink=f(sink),
                  lambda_vec=f(lambda_vec), sub_norm_gain=f(sub_norm_gain), rel_bias=f(rel_bias), **_CONSTS)
    x = f(x)
    nc = build_program()
    in_maps = [dict(x=x[i], **shared) for i in range(8)]
    res = run_bass_kernel_spmd(nc, in_maps, core_ids=list(range(8)))
    return np.stack([np.asarray(r["out"], dtype=np.float32) for r in res.results], axis=0)
```

```python
import math
import os
import contextlib
import numpy as np
import concourse.bass as bass
import concourse.mybir as mybir
from concourse.bass_utils import run_bass_kernel_spmd

F32 = mybir.dt.float32
BF16 = mybir.dt.bfloat16
AF = mybir.ActivationFunctionType
ALU = mybir.AluOpType
AX = mybir.AxisListType

SEQ = 4096
DM = 1024
NT = SEQ // 128
EPS = 1e-6
NEG = -30000.0
COL = dict(a_q=0, a_k=512, a_v=640, b_q=768, b_k=1280, b_v=1792, c_q=2304, c_k=2816, c_v=2944,
           d_q=3072, d_k=4608, d_v=6144, gate=7680, merge=9728)
D_R = (1, 4, 16)
VEC_N = 2304
STRIP_W = 2176


class Buf:
    __slots__ = ("name", "last_w", "rd_ops", "rd_dma", "dsem", "dcount", "shared", "wr")

    def __init__(self, name, shared=False):
        self.name = name
        self.last_w = None
        self.rd_ops = {}
        self.rd_dma = {}
        self.dsem = None
        self.dcount = 0
        self.shared = shared
        self.wr = {}


class Op:
    __slots__ = ("eng", "fn", "deps", "kind", "signal", "sem", "val", "dticket")

    def __init__(self, eng, fn, kind):
        self.eng = eng
        self.fn = fn
        self.kind = kind
        self.deps = []
        self.signal = False
        self.sem = None
        self.val = 0
        self.dticket = None


class Sched:
    ENGS = ("sp", "act", "pool", "pe", "dve")

    def __init__(self, nc, stack):
        self.nc = nc
        self.stack = stack
        self.streams = {e: [] for e in self.ENGS}
        self.cur_sem = {}
        self.nsem = 0
        self.all_dma = {}
        self.pending = {}
        self.sem_objs = []
        self.dsem_bufs = []
        self.free_dsems = []
        self.heavy = []

    def new_sem(self, name):
        s = self.stack.enter_context(self.nc.semaphore(f"{name}_{self.nsem}"))
        self.nsem += 1
        self.sem_objs.append(s)
        return (len(self.sem_objs) - 1, s)

    def new_epoch(self):
        self.cur_sem = {}

    def _esem(self, eng):
        if eng not in self.cur_sem:
            self.cur_sem[eng] = self.new_sem("e" + eng)
        return self.cur_sem[eng]

    def _collect(self, op, reads, writes, sync=None):
        deps = []
        for b in reads:
            if b.shared:
                deps.extend(b.wr.values())
            elif b.last_w is not None:
                deps.append(b.last_w)
        for b in writes:
            if not b.shared and b.last_w is not None:
                lw = b.last_w
                same = (op.kind == "dma" and isinstance(lw, tuple) and sync is not None and sync.dsem is not None
                        and lw[0][0] == sync.dsem[0] and not b.rd_ops and not b.rd_dma)
                if not same:
                    deps.append(lw)
            deps.extend(b.rd_ops.values())
            deps.extend(b.rd_dma.values())
        pb = self.pending.pop(op.eng, None)
        if pb is not None:
            deps.append(pb)
        out = []
        seen = set()
        for d in deps:
            if d is op:
                continue
            if isinstance(d, Op):
                if d.eng == "pe" and op.eng == "pe" and op.kind == "c":
                    continue
                if id(d) in seen:
                    continue
                seen.add(id(d))
                d.signal = True
            out.append(d)
        op.deps = out

    def add(self, eng, fn, r=(), w=()):
        op = Op(eng, fn, "c")
        op.sem = self._esem(eng)
        self._collect(op, r, w)
        for b in w:
            b.last_w = op
            b.rd_ops = {}
            b.rd_dma = {}
        for b in r:
            if b not in w:
                b.rd_ops[eng] = op
        self.streams[eng].append(op)
        return op

    def pe(self, fn, r=(), w=()):
        return self.add("pe", fn, r, w)

    def act(self, fn, r=(), w=()):
        return self.add("act", fn, r, w)

    def dve(self, fn, r=(), w=()):
        return self.add("dve", fn, r, w)

    def pool(self, fn, r=(), w=()):
        return self.add("pool", fn, r, w)

    def dma(self, eng, out, in_, r, w, heavy=False, **kw):
        op = Op(eng, lambda e: e.dma_start(out=out, in_=in_, **kw), "dma")
        sync = w if not w.shared else [b for b in r if not b.shared][0]
        self._collect(op, r, [w], sync)
        if heavy:
            if len(self.heavy) >= 2:
                op.deps.append(self.heavy[-2])
        if sync.dsem is None:
            if self.free_dsems:
                sync.dsem, sync.dcount = self.free_dsems.pop()
            else:
                sync.dsem = self.new_sem("d" + sync.name)
            self.dsem_bufs.append(sync)
        sync.dcount += 16
        t = (sync.dsem, sync.dcount)
        op.dticket = t
        self.all_dma[sync.dsem[0]] = t
        w.rd_ops = {}
        w.rd_dma = {}
        if w.shared:
            w.wr[sync.dsem[0]] = t
        else:
            w.last_w = t
        for b in r:
            b.rd_dma[sync.dsem[0]] = t
        if heavy:
            self.heavy.append(t)
        self.streams[eng].append(op)
        return op

    def mark(self):
        return len(self.dsem_bufs)

    def recycle(self, mark):
        for b in self.dsem_bufs[mark:]:
            self.free_dsems.append((b.dsem, b.dcount))
            b.dsem = None
        del self.dsem_bufs[mark:]

    def barrier(self):
        op = Op("sp", None, "bar")
        op.sem = self._esem("sp")
        deps = []
        for e in self.ENGS:
            if e == "sp":
                continue
            st = [o for o in self.streams[e] if o.kind == "c"]
            if st:
                st[-1].signal = True
                deps.append(st[-1])
        deps.extend(self.all_dma.values())
        pb = self.pending.pop("sp", None)
        if pb is not None:
            deps.append(pb)
        op.deps = deps
        op.signal = True
        self.streams["sp"].append(op)
        for e in self.ENGS:
            if e != "sp":
                self.pending[e] = op
        return op

    def finish(self, bufs):
        op = Op("sp", None, "wait")
        op.deps = []
        for b in bufs:
            op.deps.extend(b.wr.values())
        self.streams["sp"].append(op)

    def finalize(self):
        for e in self.ENGS:
            cnt = {}
            for op in self.streams[e]:
                if op.kind in ("c", "bar") and op.signal:
                    k = op.sem[0]
                    cnt[k] = cnt.get(k, 0) + 1
                    op.val = cnt[k]

    def emit(self, eng, e):
        waited = {}
        for op in self.streams[eng]:
            for d in op.deps:
                if isinstance(d, Op):
                    sem, val = d.sem, d.val
                else:
                    sem, val = d
                if waited.get(sem[0], 0) >= val:
                    continue
                e.wait_ge(sem[1], val)
                waited[sem[0]] = val
            if op.kind == "c":
                ins = op.fn(e)
                if op.signal:
                    ins.then_inc(op.sem[1], 1)
            elif op.kind == "dma":
                ins = op.fn(e)
                ins.then_inc(op.dticket[0][1], 16)
            elif op.kind == "bar":
                e.sem_inc(op.sem[1], 1)


class Arena:
    def __init__(self, ap, nbytes, sched=None):
        self.sched = sched
        self.ap = ap
        self.nbytes = nbytes
        self.off = 0
        self.stack = []
        self.peak = 0

    def push(self):
        self.stack.append((self.off, self.sched.mark()))

    def pop(self):
        self.off, mk = self.stack.pop()
        self.sched.recycle(mk)

    def alloc(self, shape, dtype, parts=128):
        esz = 2 if dtype == BF16 else 4
        n = 1
        for s in shape:
            n *= s
        nb = (n * esz + 63) // 64 * 64
        assert self.off + nb <= self.nbytes, f"arena overflow {self.off}+{nb}"
        a = self.ap[0:parts, self.off // 4:(self.off + nb) // 4]
        self.off += nb
        self.peak = max(self.peak, self.off)
        if dtype != F32:
            a = a.bitcast(dtype)
        a = a[:, 0:n]
        if len(shape) == 2:
            a = a.rearrange("p (a b) -> p a b", a=shape[0])
        elif len(shape) == 3:
            a = a.rearrange("p (a b c) -> p a b c", a=shape[0], b=shape[1])
        return a


class Ring:
    def __init__(self, arena, name, n, shape, dtype, parts=128):
        self.items = [(arena.alloc(shape, dtype, parts), Buf(f"{name}{i}")) for i in range(n)]
        self.i = 0

    def next(self):
        it = self.items[self.i % len(self.items)]
        self.i += 1
        return it


def _rel_bucket_np(rel):
    rel = np.asarray(rel, dtype=np.int64)
    n = np.abs(rel)
    large = 8 + (np.log(np.maximum(n, 8).astype(np.float32) / np.float32(8)) / np.float32(math.log(128.0)) * 8).astype(np.int32)
    large = np.minimum(large, 15)
    return np.where(rel > 0, 16, 0) + np.where(n < 8, n, large)


def make_consts():
    c = {}
    c["c_ident"] = np.eye(128, dtype=np.float32)
    bo = np.zeros((128, 128), np.float32)
    bo[:64, :64] = 1.0
    bo[64:, 64:] = 1.0
    c["c_blockones"] = bo
    c["c_flip"] = np.ascontiguousarray(np.eye(128, dtype=np.float32)[::-1])
    oh = np.zeros((5, 32, VEC_N), np.float32)
    mk = np.zeros((5, VEC_N), np.float32)
    i = np.arange(511)
    dl = 255 - i
    oh[0][_rel_bucket_np(dl), i] = 1.0
    mk[0, :511] = np.where(np.abs(dl) <= 128, 0.0, NEG)
    for g, r in enumerate(D_R):
        i = np.arange(383)
        dl = 191 - i
        oh[1 + g][_rel_bucket_np(dl * r), i] = 1.0
        mk[1 + g, :383] = np.where(np.abs(dl) <= 64, 0.0, NEG)
    i = np.arange(2303)
    dl = 1151 - i
    oh[4][_rel_bucket_np(dl), i] = 1.0
    c["c_onehot"] = oh
    c["c_mask"] = mk
    nf = 16
    freqs = (np.float32(10000.0) ** (-np.arange(nf, dtype=np.float32) / np.float32(nf))).astype(np.float32)
    t = np.arange(SEQ)
    row = (t // 64).astype(np.float32)
    colp = (t % 64).astype(np.float32)
    ang = np.concatenate([row[:, None] * freqs[None, :], colp[:, None] * freqs[None, :]], axis=-1).astype(np.float32)
    cos = np.cos(ang).astype(np.float32).T
    sin = np.sin(ang).astype(np.float32).T
    c["c_cos"] = np.ascontiguousarray(np.tile(cos, (4, 1)))
    c["c_sin"] = np.ascontiguousarray(np.tile(sin, (4, 1)))
    sg = np.where((np.arange(128) % 64) < 32, -1.0, 1.0).astype(np.float32)
    c["c_sign"] = sg.reshape(128, 1)
    return c


def build_program(layers=(0, 1), phases="PABCDF", debug=False):
    nc = bass.Bass("TRN2", target_bir_lowering=False)
    okind = "ExternalOutput" if debug else "Internal"

    def din(name, shape):
        return nc.dram_tensor(name, list(shape), F32, kind="ExternalInput")

    x_t = din("x", [SEQ, DM])
    w_in_t = din("w_in", [2, DM, 13824])
    w_br_t = din("w_branch", [2, 4, 512, DM])
    w_out_t = din("w_out", [2, DM, DM])
    ng_t = din("norm_gain", [2, DM])
    qg_t = din("qk_gain", [2, 4, 2, 64])
    sink_t = din("sink", [2, 8])
    lv_t = din("lambda_vec", [2, 4, 64])
    sg_t = din("sub_norm_gain", [2, 128])
    rb_t = din("rel_bias", [32, 36])
    c_ident = din("c_ident", [128, 128])
    c_bones = din("c_blockones", [128, 128])
    c_flip = din("c_flip", [128, 128])
    c_oh = din("c_onehot", [5, 32, VEC_N])
    c_mask = din("c_mask", [5, VEC_N])
    c_cos = din("c_cos", [128, SEQ])
    c_sin = din("c_sin", [128, SEQ])
    c_sign = din("c_sign", [128, 1])

    out_t = nc.dram_tensor("out", [SEQ, DM], F32, kind="ExternalOutput")
    xmid_t = nc.dram_tensor("xmid", [SEQ, DM], F32, kind=okind)
    hT_t = nc.dram_tensor("hT_d", [DM, SEQ], BF16, kind=okind)
    qkA_t = nc.dram_tensor("qkA", [640, SEQ], BF16, kind=okind)
    qkB_t = nc.dram_tensor("qkB", [1024, SEQ], BF16, kind=okind)
    qkC_t = nc.dram_tensor("qkC", [640, SEQ], BF16, kind=okind)
    qkD_t = nc.dram_tensor("qkD", [3072, SEQ], BF16, kind=okind)
    vAC_t = nc.dram_tensor("vAC", [SEQ, 256], BF16, kind=okind)
    vB_t = nc.dram_tensor("vB", [SEQ, 512], BF16, kind=okind)
    vD_t = nc.dram_tensor("vD", [3, SEQ, 512], BF16, kind=okind)
    gated_t = nc.dram_tensor("gated", [SEQ, 2048], BF16, kind=okind)
    dacc_t = nc.dram_tensor("dacc", [3, SEQ, 520], F32, kind=okind)
    vec_t = nc.dram_tensor("vec_d", [36, VEC_N], F32, kind=okind)
    bt_t = nc.dram_tensor("btile", [36, 128, STRIP_W], F32, kind=okind)

    x_ap, xmid_ap, out_ap = x_t.ap(), xmid_t.ap(), out_t.ap()
    w_in, w_br, w_out = w_in_t.ap(), w_br_t.ap(), w_out_t.ap()
    hT_d, gated_d = hT_t.ap(), gated_t.ap()
    qk_d = dict(A=qkA_t.ap(), B=qkB_t.ap(), C=qkC_t.ap(), D=qkD_t.ap())
    vAC_d, vB_d, vD_d, dacc_d = vAC_t.ap(), vB_t.ap(), vD_t.ap(), dacc_t.ap()
    vec_d, bt_d = vec_t.ap(), bt_t.ap()

    ARENA_BYTES = 200 * 1024
    stack = contextlib.ExitStack()
    with stack:
        arena_t = stack.enter_context(nc.sbuf_tensor("arena", [128, ARENA_BYTES // 4], F32))
        banks = [stack.enter_context(nc.psum_tensor(f"bank{i}", [128, 512], F32)) for i in range(8)]
        S = Sched(nc, stack)
        ar = Arena(arena_t, ARENA_BYTES, S)
        PB = [Buf(f"ps{i}") for i in range(8)]

        def bk(i):
            return banks[i][:, :]

        B_x = Buf("x", True)
        B_xmid = Buf("xmid", True)
        B_out = Buf("out", True)
        B_hT = Buf("hTd", True)
        B_qk = {k: Buf("qk" + k, True) for k in "ABCD"}
        B_vAC, B_vB, B_vD = Buf("vAC", True), Buf("vB", True), Buf("vD", True)
        B_gated, B_dacc = Buf("gated", True), Buf("dacc", True)
        B_vec, B_bt = Buf("vecd", True), Buf("btd", True)
        B_const = Buf("constin", True)

        ident_bf = ar.alloc([128], BF16)
        bones_bf = ar.alloc([128], BF16)
        sign_sb = ar.alloc([1], F32)
        cfar_sb = ar.alloc([8], F32)
        Bk_const = Buf("kconst")
        S.dma("pool", ident_bf, c_ident.ap(), [B_const], Bk_const)
        Bk2 = Buf("kconst2")
        S.dma("pool", bones_bf, c_bones.ap(), [B_const], Bk2)
        Bk3 = Buf("kconst3")
        S.dma("sp", sign_sb, c_sign.ap(), [B_const], Bk3)
        B_cfar = Buf("cfar")

        def phase_setup():
            ar.push()
            rb_sb = ar.alloc([36], F32, parts=32)
            B_rb = Buf("rb")
            S.dma("sp", rb_sb, rb_t.ap(), [B_const], B_rb)
            flip_sb = ar.alloc([128], F32)
            B_flip = Buf("flip")
            S.dma("sp", flip_sb, c_flip.ap(), [B_const], B_flip)
            oh_ring = Ring(ar, "oh", 2, [VEC_N], F32, parts=32)
            mk_ring = Ring(ar, "mk", 2, [VEC_N], F32, parts=32)
            vs_ring = Ring(ar, "vs", 2, [VEC_N], F32, parts=32)
            fams = [(0, 0, 8, 511), (1, 12, 8, 383), (2, 20, 8, 383), (3, 28, 8, 383), (4, 8, 4, 2303)]
            pi = 0
            for seg, c0, nh, n in fams:
                oh_sb, B_oh = oh_ring.next()
                mk_sb, B_mk = mk_ring.next()
                vs_sb, B_vs = vs_ring.next()
                S.dma("sp", oh_sb[:, 0:n], c_oh.ap()[seg, :, 0:n], [B_const], B_oh)
                msrc = bass.AP(c_mask, seg * VEC_N, [[0, nh], [1, n]])
                S.dma("sp", mk_sb[0:nh, 0:n], msrc, [B_const], B_mk)
                for a in range(0, n, 512):
                    wdt = min(512, n - a)
                    pb = pi % 2
                    pi += 1
                    S.pe(lambda e, pb=pb, c0=c0, nh=nh, a=a, wdt=wdt, oh_sb=oh_sb: e.matmul(
                        bk(pb)[0:nh, 0:wdt], lhsT=rb_sb[:, c0:c0 + nh], rhs=oh_sb[:, a:a + wdt], start=True, stop=True),
                        r=[B_rb, B_oh], w=[PB[pb]])
                    S.dve(lambda e, pb=pb, nh=nh, a=a, wdt=wdt, vs_sb=vs_sb, mk_sb=mk_sb: e.tensor_tensor(
                        out=vs_sb[0:nh, a:a + wdt], in0=bk(pb)[0:nh, 0:wdt], in1=mk_sb[0:nh, a:a + wdt], op=ALU.add),
                        r=[PB[pb], B_mk], w=[B_vs])
                S.dma("sp", vec_d[c0:c0 + nh, 0:n], vs_sb[0:nh, 0:n], [B_vs], B_vec)
            hk_ring = Ring(ar, "hk", 2, [STRIP_W], F32)
            tz_ring = Ring(ar, "tz", 2, [STRIP_W], F32)
            for h36 in range(36):
                W = 384 if h36 < 8 else (STRIP_W if h36 < 12 else 256)
                hk_sb, B_hk = hk_ring.next()
                tz_sb, B_tz = tz_ring.next()
                src = bass.AP(vec_t, h36 * VEC_N, [[1, 128], [1, W]])
                S.dma("sp", hk_sb[:, 0:W], src, [B_vec], B_hk)
                for a in range(0, W, 512):
                    wdt = min(512, W - a)
                    pb = pi % 2
                    pi += 1
                    S.pe(lambda e, pb=pb, a=a, wdt=wdt, hk_sb=hk_sb: e.matmul(
                        bk(pb)[:, 0:wdt], lhsT=flip_sb, rhs=hk_sb[:, a:a + wdt], start=True, stop=True),
                        r=[B_flip, B_hk], w=[PB[pb]])
                    S.dve(lambda e, pb=pb, a=a, wdt=wdt, tz_sb=tz_sb: e.tensor_copy(
                        out=tz_sb[:, a:a + wdt], in_=bk(pb)[:, 0:wdt]), r=[PB[pb]], w=[B_tz])
                S.dma("sp", bt_d[h36, :, 0:W], tz_sb[:, 0:W], [B_tz], B_bt)
            for h in range(4):
                for j, idx in enumerate((0, 2302)):
                    src = bass.AP(vec_t, (8 + h) * VEC_N + idx, [[0, 128], [1, 1]])
                    S.dma("sp", cfar_sb[:, 2 * h + j:2 * h + j + 1], src, [B_vec], B_cfar)
            S.barrier()
            ar.pop()

        phase_setup()

        for l in layers:
            S.new_epoch()
            x_src, B_xs = (x_ap, B_x) if l == 0 else (xmid_ap, B_xmid)
            x_dst, B_xd = (xmid_ap, B_xmid) if (l == 0 and len(layers) > 1 and layers[-1] != 0) else (out_ap, B_out)
            lam_init = 0.8 - 0.6 * math.exp(-0.3 * l)
            wl = w_in[l]

            ar.push()
            lcraw = ar.alloc([10], F32)
            lc = ar.alloc([10], F32)
            esink = ar.alloc([8], F32)
            lvb = ar.alloc([256], F32)
            ltmp = ar.alloc([128], F32)
            lsc = ar.alloc([8], F32)
            subg = ar.alloc([128], F32)
            B_lcraw, B_lc, B_esink, B_lvb, B_ltmp, B_lsc, B_subg = (Buf(n) for n in ("lcraw", "lc", "esink", "lvb", "ltmp", "lsc", "subg"))
            qg = qg_t
            for half in range(2):
                src = bass.AP(qg, l * 512, [[1, 64], [64, 8]])
                S.dma("sp", lcraw[64 * half:64 * half + 64, 0:8], src, [B_const], B_lcraw, allow_slow_non_contiguous=True)
            for s in range(2):
                base = l * 512 + (2 * 2 + s) * 64
                for q4 in range(4):
                    so = 32 if q4 % 2 == 0 else 0
                    src = bass.AP(qg, base + so, [[1, 32], [1, 1]])
                    S.dma("sp", lcraw[32 * q4:32 * q4 + 32, 8 + s:9 + s], src, [B_const], B_lcraw)
            for m in range(4):
                S.dve(lambda e, m=m: e.tensor_scalar(out=lc[:, m:m + 1], in0=lcraw[:, 2 * m:2 * m + 1], scalar1=0.125, scalar2=None, op0=ALU.mult),
                      r=[B_lcraw], w=[B_lc])
                S.dve(lambda e, m=m: e.tensor_copy(out=lc[:, 4 + m:5 + m], in_=lcraw[:, 2 * m + 1:2 * m + 2]), r=[B_lcraw], w=[B_lc])
            S.dve(lambda e: e.tensor_scalar(out=lc[:, 8:9], in0=lcraw[:, 8:9], scalar1=sign_sb[:, 0:1], scalar2=0.125, op0=ALU.mult, op1=ALU.mult),
                  r=[B_lcraw, Bk3], w=[B_lc])
            S.dve(lambda e: e.tensor_scalar(out=lc[:, 9:10], in0=lcraw[:, 9:10], scalar1=sign_sb[:, 0:1], scalar2=None, op0=ALU.mult),
                  r=[B_lcraw, Bk3], w=[B_lc])
            S.dma("sp", esink, bass.AP(sink_t, l * 8, [[0, 128], [1, 8]]), [B_const], B_esink)
            S.act(lambda e: e.activation(out=esink, in_=esink, func=AF.Exp), r=[B_esink], w=[B_esink])
            S.dma("sp", lvb, bass.AP(lv_t, l * 256, [[0, 128], [1, 256]]), [B_const], B_lvb)
            S.dve(lambda e: e.tensor_tensor(out=ltmp[:, 0:64], in0=lvb[:, 0:64], in1=lvb[:, 64:128], op=ALU.mult), r=[B_lvb], w=[B_ltmp])
            S.dve(lambda e: e.tensor_tensor(out=ltmp[:, 64:128], in0=lvb[:, 128:192], in1=lvb[:, 192:256], op=ALU.mult), r=[B_lvb], w=[B_ltmp])
            S.dve(lambda e: e.tensor_reduce(out=lsc[:, 0:2], in_=ltmp.rearrange("p (a b) -> p a b", a=2), axis=AX.X, op=ALU.add), r=[B_ltmp], w=[B_lsc])
            S.act(lambda e: e.activation(out=lsc[:, 2:4], in_=lsc[:, 0:2], func=AF.Exp), r=[B_lsc], w=[B_lsc])
            S.dve(lambda e: e.tensor_tensor(out=lsc[:, 4:5], in0=lsc[:, 3:4], in1=lsc[:, 2:3], op=ALU.subtract), r=[B_lsc], w=[B_lsc])
            S.dve(lambda e, li=lam_init: e.tensor_scalar(out=lsc[:, 5:6], in0=lsc[:, 4:5], scalar1=-li, scalar2=None, op0=ALU.add), r=[B_lsc], w=[B_lsc])
            neglam = lsc[:, 5:6]
            S.dma("sp", subg, bass.AP(sg_t, l * 128, [[0, 128], [1, 128]]), [B_const], B_subg)
            S.dve(lambda e, li=lam_init: e.tensor_scalar(out=subg, in0=subg, scalar1=1.0 - li, scalar2=None, op0=ALU.mult), r=[B_subg], w=[B_subg])

            def phase_P():
                ar.push()
                hT_sb = ar.alloc([8, SEQ], BF16)
                B_hTsb = Buf("hTsb")
                gain_bc = ar.alloc([DM], F32)
                B_gain = Buf("gainbc")
                S.dma("sp", gain_bc, bass.AP(ng_t, l * DM, [[0, 128], [1, DM]]), [B_const], B_gain)
                ar.push()
                x_ring = Ring(ar, "xt", 2, [DM], F32)
                hb_ring = Ring(ar, "hb", 2, [DM], BF16)
                junk = ar.alloc([DM], BF16)
                B_junk = Buf("junk")
                st_ring = Ring(ar, "st", 2, [4], F32)
                for tt in range(NT):
                    xt, B_xt = x_ring.next()
                    hb, B_hb = hb_ring.next()
                    st, B_st = st_ring.next()
                    pb = tt % 2
                    S.dma("sp", xt, x_src[tt * 128:(tt + 1) * 128, :], [B_xs], B_xt)
                    S.act(lambda e, xt=xt, st=st: e.activation(out=junk, in_=xt, func=AF.Square, accum_out=st[:, 0:1]), r=[B_xt], w=[B_junk, B_st])
                    S.act(lambda e, st=st: e.activation(out=st[:, 1:2], in_=st[:, 0:1], func=AF.Ln, scale=1.0 / DM, bias=EPS), r=[B_st], w=[B_st])
                    S.act(lambda e, st=st: e.activation(out=st[:, 2:3], in_=st[:, 1:2], func=AF.Exp, scale=-0.5), r=[B_st], w=[B_st])
                    S.dve(lambda e, xt=xt, st=st, hb=hb: e.scalar_tensor_tensor(out=hb, in0=xt, scalar=st[:, 2:3], in1=gain_bc, op0=ALU.mult, op1=ALU.mult),
                          r=[B_xt, B_st, B_gain], w=[B_hb])
                    tp = bk(pb).bitcast(BF16)
                    for kc in range(8):
                        S.pe(lambda e, tp=tp, hb=hb, kc=kc: e.transpose(out=tp[:, kc * 128:(kc + 1) * 128], in_=hb[:, kc * 128:(kc + 1) * 128], identity=ident_bf),
                             r=[B_hb, Bk_const], w=[PB[pb]])
                    S.dve(lambda e, tp=tp, tt=tt: e.tensor_copy(out=hT_sb[:, :, tt * 128:(tt + 1) * 128], in_=tp.rearrange("p (a b) -> p a b", a=8)),
                          r=[PB[pb]], w=[B_hTsb])
                for kc in range(8):
                    S.dma("sp", hT_d[kc * 128:(kc + 1) * 128, :], hT_sb[:, kc, :], [B_hTsb], B_hT)
                S.barrier()
                ar.pop()

                ar.push()
                w_ring = Ring(ar, "wc", 3, [8, 128], BF16)
                sq_ring = Ring(ar, "sq", 4, [512], BF16)
                ln_ring = Ring(ar, "ln", 2, [512], F32)
                rs_ring = Ring(ar, "rs", 2, [512], F32)
                y_ring = Ring(ar, "y", 3, [512], BF16)
                vo_ring = Ring(ar, "vo", 4, [512], BF16)
                wvs = ar.alloc([8, 512], BF16)
                B_wvs = Buf("wvs")
                pstate = dict(mi=0, vi=0)

                def run_chunks(chunks, src, B_src, rope_res=None):
                    wres = []
                    for ch in chunks:
                        wres.append((w_ring.next(), rope_res["wp"].next() if ch["rope"] is not None else None))

                    def load_w(ci):
                        ch = chunks[ci]
                        (wc, B_wc), wpp = wres[ci]
                        c0 = ch["c0"]
                        S.dma("pool", wc, wl[:, c0:c0 + 128].rearrange("(kc p) c -> p kc c", p=128), [B_const], B_wc, heavy=True)
                        if wpp is not None:
                            wp, B_wp = wpp
                            for q4 in range(4):
                                so = (32 if q4 % 2 == 0 else -32) + 32 * q4
                                S.dma("pool", wp[:, :, 32 * q4:32 * q4 + 32], wl[:, c0 + so:c0 + so + 32].rearrange("(kc p) c -> p kc c", p=128), [B_const], B_wp, heavy=True)

                    def mk_ptile(ci, tc, mi):
                        ch = chunks[ci]
                        (wc, B_wc), wpp = wres[ci]
                        pq = mi % 4
                        pp = 4 + (mi % 3)
                        psn = 7
                        sq, B_sq = sq_ring.next()
                        lnv, B_ln = ln_ring.next()
                        rs, B_rs = rs_ring.next()
                        y, B_y = y_ring.next()
                        g = ch["g"]
                        if ch["rope"] is not None:
                            wp, B_wp = wpp
                            gp = ch["rope"]
                            y1, B_y1 = rope_res["y1"].next()
                            y2, B_y2 = rope_res["y2"].next()
                            cs, B_cs = rope_res["cs"].next()

                        def sa():
                            if tc == 0 and ci + 2 < len(chunks):
                                load_w(ci + 2)
                            for kc in range(8):
                                S.pe(lambda e, kc=kc: e.matmul(bk(pq), lhsT=wc[:, kc, :], rhs=src[:, kc, tc * 512:(tc + 1) * 512], start=(kc == 0), stop=(kc == 7)),
                                     r=[B_wc, B_src], w=[PB[pq]])
                            if ch["rope"] is not None:
                                for kc in range(8):
                                    S.pe(lambda e, kc=kc: e.matmul(bk(pp), lhsT=wp[:, kc, :], rhs=src[:, kc, tc * 512:(tc + 1) * 512], start=(kc == 0), stop=(kc == 7)),
                                         r=[B_wp, B_src], w=[PB[pp]])
                                S.dma("sp", cs[:, 0, :], c_cos.ap()[:, tc * 512:(tc + 1) * 512], [B_const], B_cs)
                                S.dma("sp", cs[:, 1, :], c_sin.ap()[:, tc * 512:(tc + 1) * 512], [B_const], B_cs)
                            S.act(lambda e: e.activation(out=sq, in_=bk(pq), func=AF.Square), r=[PB[pq]], w=[B_sq])

                        def sb_():
                            S.pe(lambda e: e.matmul(bk(psn), lhsT=bones_bf, rhs=sq, start=True, stop=True), r=[B_sq, Bk2], w=[PB[psn]])
                            S.act(lambda e: e.activation(out=lnv, in_=bk(psn), func=AF.Ln, scale=1.0 / 64, bias=EPS), r=[PB[psn]], w=[B_ln])
                            S.act(lambda e: e.activation(out=rs, in_=lnv, func=AF.Exp, scale=-0.5), r=[B_ln], w=[B_rs])
                            if ch["rope"] is None:
                                S.dve(lambda e: e.scalar_tensor_tensor(out=y, in0=bk(pq), scalar=lc[:, g:g + 1], in1=rs, op0=ALU.mult, op1=ALU.mult),
                                      r=[PB[pq], B_rs, B_lc], w=[B_y])
                            else:
                                S.dve(lambda e: e.scalar_tensor_tensor(out=y1, in0=bk(pq), scalar=lc[:, g:g + 1], in1=rs, op0=ALU.mult, op1=ALU.mult),
                                      r=[PB[pq], B_rs, B_lc], w=[B_y1])
                                S.dve(lambda e: e.scalar_tensor_tensor(out=y2, in0=bk(pp), scalar=lc[:, gp:gp + 1], in1=rs, op0=ALU.mult, op1=ALU.mult),
                                      r=[PB[pp], B_rs, B_lc], w=[B_y2])
                                S.pool(lambda e: e.tensor_tensor(out=y1, in0=y1, in1=cs[:, 0, :], op=ALU.mult), r=[B_y1, B_cs], w=[B_y1])
                                S.pool(lambda e: e.tensor_tensor(out=y2, in0=y2, in1=cs[:, 1, :], op=ALU.mult), r=[B_y2, B_cs], w=[B_y2])
                                S.dve(lambda e: e.tensor_tensor(out=y, in0=y1, in1=y2, op=ALU.add), r=[B_y1, B_y2], w=[B_y])
                            S.dma("pool", ch["dst"][ch["row"]:ch["row"] + 128, tc * 512:(tc + 1) * 512], y, [B_y], ch["B"])
                        return (sa, sb_)

                    ptiles = []
                    for ci in range(len(chunks)):
                        for tc in range(8):
                            ptiles.append(mk_ptile(ci, tc, pstate["mi"]))
                            pstate["mi"] += 1
                    for ci in range(min(2, len(chunks))):
                        load_w(ci)
                    n_pt = len(ptiles)
                    for i in range(n_pt + 2):
                        if i >= 2:
                            ptiles[i - 2][1]()
                        if i < n_pt:
                            ptiles[i][0]()

                def run_vjobs(jobs, src, B_src):
                    for tt in range(NT):
                        for (wap, B_w, n, dst, Bd) in jobs:
                            pb = pstate["vi"] % 4
                            pstate["vi"] += 1
                            vo, B_vo = vo_ring.next()
                            for kc in range(8):
                                S.pe(lambda e, pb=pb, n=n, kc=kc, wap=wap, tt=tt: e.matmul(bk(pb)[:, 0:n], lhsT=src[:, kc, tt * 128:(tt + 1) * 128], rhs=wap[:, kc, 0:n],
                                                                                     start=(kc == 0), stop=(kc == 7)), r=[B_src, B_w], w=[PB[pb]])
                            if pstate["vi"] % 2 == 0:
                                S.act(lambda e, vo=vo, pb=pb, n=n: e.copy(out=vo[:, 0:n], in_=bk(pb)[:, 0:n]), r=[PB[pb]], w=[B_vo])
                            else:
                                S.dve(lambda e, vo=vo, pb=pb, n=n: e.tensor_copy(out=vo[:, 0:n], in_=bk(pb)[:, 0:n]), r=[PB[pb]], w=[B_vo])
                            S.dma("pool", dst[tt * 128:(tt + 1) * 128, :], vo[:, 0:n], [B_vo], Bd)

                def load_wvs(g3):
                    c0 = COL["d_v"] + 512 * g3
                    S.dma("pool", wvs, wl[:, c0:c0 + 512].rearrange("(kc p) c -> p kc c", p=128), [B_const], B_wvs, heavy=True)

                chunks = []
                for i in range(5):
                    chunks.append(dict(c0=COL["a_q"] + 128 * i, dst=qk_d["A"], row=128 * i, g=0 if i < 4 else 4, r=1, rope=None, B=B_qk["A"]))
                for i in range(8):
                    chunks.append(dict(c0=COL["b_q"] + 128 * i, dst=qk_d["B"], row=128 * i, g=1 if i < 4 else 5, r=1, rope=None, B=B_qk["B"]))
                for i in range(5):
                    chunks.append(dict(c0=COL["c_q"] + 128 * i, dst=qk_d["C"], row=128 * i, g=2 if i < 4 else 6, r=1, rope=8 if i < 4 else 9, B=B_qk["C"]))
                for i in range(24):
                    grp = (i % 12) // 4
                    chunks.append(dict(c0=COL["d_q"] + 128 * i, dst=qk_d["D"], row=128 * i, g=3 if i < 12 else 7, r=D_R[grp], rope=None, B=B_qk["D"]))

                ar.push()
                rope_res = dict(wp=Ring(ar, "wp", 3, [8, 128], BF16), y1=Ring(ar, "y1", 2, [512], F32), y2=Ring(ar, "y2", 2, [512], F32),
                                cs=Ring(ar, "cs", 4, [2, 512], F32))
                wv0 = ar.alloc([8, 768], BF16)
                B_wv0 = Buf("wv0")
                for (c0, n, o) in [(COL["a_v"], 128, 0), (COL["c_v"], 128, 128), (COL["b_v"], 512, 256)]:
                    S.dma("pool", wv0[:, :, o:o + n], wl[:, c0:c0 + n].rearrange("(kc p) c -> p kc c", p=128), [B_const], B_wv0, heavy=True)
                load_wvs(0)
                run_chunks([c for c in chunks if c["r"] == 1], hT_sb, B_hTsb, rope_res)
                run_vjobs([(wv0[:, :, 0:256], B_wv0, 256, vAC_d, B_vAC), (wv0[:, :, 256:768], B_wv0, 512, vB_d, B_vB), (wvs, B_wvs, 512, vD_d[0], B_vD)], hT_sb, B_hTsb)
                S.barrier()
                ar.pop()
                ar.push()
                hT_p = ar.alloc([8, SEQ], BF16)
                B_hTp = Buf("hTp")
                for g3 in (1, 2):
                    r = D_R[g3]
                    for kc in range(8):
                        eng = S.dve if kc % 2 == 0 else S.pool
                        eng(lambda e, kc=kc, r=r: e.tensor_copy(out=hT_p[:, kc, :].rearrange("p (c m) -> p c m", c=r), in_=hT_sb[:, kc, :].rearrange("p (m c) -> p c m", c=r)),
                            r=[B_hTsb], w=[B_hTp])
                    load_wvs(g3)
                    run_chunks([c for c in chunks if c["r"] == r], hT_p, B_hTp)
                    run_vjobs([(wvs, B_wvs, 512, vD_d[g3], B_vD)], hT_p, B_hTp)
                S.barrier()
                ar.pop()
                ar.pop()
                ar.pop()

            class Epi:
                def __init__(self, n, width, pbank):
                    self.n, self.c0, self.width, self.pbank = n, 0, width, pbank
                    self.nq = 512 // width
                    self.wg = ar.alloc([8, width], BF16)
                    self.B_wg = Buf("wg")
                    self.hq_ring = Ring(ar, "hq", 2, [8, 512], BF16)
                    self.sg_ring = Ring(ar, "sg", 2, [512], F32)
                    self.gt_ring = Ring(ar, "gt", 2, [4, width], BF16)

                def load(self, c0):
                    self.c0 = c0
                    gc0 = COL["gate"] + self.n * 512 + c0
                    S.dma("pool", self.wg, wl[:, gc0:gc0 + self.width].rearrange("(kc p) c -> p kc c", p=128), [B_const], self.B_wg, heavy=True)

                def start(self, qt):
                    hq, B_hq = self.hq_ring.next()
                    S.dma("sp", hq, hT_d[:, qt * 512:(qt + 1) * 512].rearrange("(kc p) t -> p kc t", p=128), [B_hT], B_hq)
                    gt, B_gt = self.gt_ring.next()
                    return dict(qt=qt, hq=hq, B_hq=B_hq, gt=gt, B_gt=B_gt, c0=self.c0)

                def mm(self, cx, grp):
                    pb, wdt, hq = self.pbank, self.width, cx["hq"]
                    for s_ in range(self.nq):
                        qb = grp * self.nq + s_
                        for kc in range(8):
                            S.pe(lambda e, kc=kc, qb=qb, s_=s_: e.matmul(bk(pb)[:, s_ * wdt:(s_ + 1) * wdt], lhsT=hq[:, kc, qb * 128:(qb + 1) * 128], rhs=self.wg[:, kc, :],
                                                                       start=(kc == 0), stop=(kc == 7)), r=[cx["B_hq"], self.B_wg], w=[PB[pb]])

                def fin(self, cx, grp, o_ap, B_o):
                    pb, wdt, gt, nq = self.pbank, self.width, cx["gt"], self.nq
                    sg, B_sg = self.sg_ring.next()
                    q0 = grp * nq
                    S.act(lambda e: e.activation(out=sg, in_=bk(pb), func=AF.Silu), r=[PB[pb]], w=[B_sg])
                    S.dve(lambda e: e.tensor_tensor(out=gt[:, q0:q0 + nq, :], in0=sg.rearrange("p (a b) -> p a b", a=nq), in1=o_ap[:, q0:q0 + nq, :], op=ALU.mult),
                          r=[B_sg, B_o], w=[cx["B_gt"]])
                    if q0 + nq == 4:
                        gc = self.n * 512 + cx["c0"]
                        qt = cx["qt"]
                        S.dma("pool", gated_d[qt * 512:(qt + 1) * 512, gc:gc + wdt].rearrange("(qb p) w -> p qb w", p=128), gt, [cx["B_gt"]], B_gated)

                def run(self, qt, o_ap, B_o):
                    cx = self.start(qt)
                    for grp in range(4 // self.nq):
                        self.mm(cx, grp)
                        self.fin(cx, grp, o_ap, B_o)

            def load_v(v_sb, B_v, src, dv):
                S.dma("sp", v_sb[:, :, 0:dv], src.rearrange("(j p) e -> p j e", p=128), [B_vAC, B_vB, B_vD], B_v)

            def run_pipeline(tiles, lag, dq=None, state=None):
                dq = dq if dq is not None else []
                state = state if state is not None else {}
                n = len(tiles)
                i = 0
                while i < n + lag or dq:
                    state["i"] = i
                    if i < n:
                        tiles[i][0]()
                    if lag <= i < n + lag:
                        tiles[i - lag][1]()
                    k = 0
                    while k < len(dq):
                        if dq[k][0] <= i:
                            dq.pop(k)[1]()
                        else:
                            k += 1
                    i += 1

            SBK = [0, 1, 7]
            LAG = 2

            def phase_B():
                ar.push()
                kT_ring = Ring(ar, "kT", 2, [2, SEQ], BF16)
                for (kT, B_kT) in kT_ring.items:
                    S.pool(lambda e, kT=kT: e.memset(kT[64:128, 0, :], 0.0), w=[B_kT])
                    S.pool(lambda e, kT=kT: e.memset(kT[0:64, 1, :], 0.0), w=[B_kT])
                v_ring = Ring(ar, "vb", 2, [32, 129], BF16)
                for (v_sb, B_v) in v_ring.items:
                    S.pool(lambda e, v_sb=v_sb: e.memset(v_sb[:, :, 128:129], 1.0), w=[B_v])
                strip_ring = Ring(ar, "strip", 2, [STRIP_W], F32)
                q_ring = Ring(ar, "qb", 2, [512], BF16)
                sb_ring = Ring(ar, "sbias", 3, [512], F32)
                p_ring = Ring(ar, "pT", 5, [512], BF16)
                oc_ring = Ring(ar, "oc", 2, [2, 4, 128], F32)
                o_ring = Ring(ar, "ob", 2, [4, 128], F32)
                sm2_ring = Ring(ar, "smb", 3, [8], F32)
                rd_ring = Ring(ar, "rdb", 3, [4], F32)
                raw_ring = Ring(ar, "rawb", 3, [4, 129], F32)
                dq = []
                state = {}
                junk = ar.alloc([128], BF16)
                B_junk = Buf("junkb")
                epi = Epi(1, 128, 6)
                hres = [(kT_ring.next(), v_ring.next(), strip_ring.next()) for h in range(4)]
                qres = {(h, qt): q_ring.next() for h in range(4) for qt in range(8)}
                ocres = {(h, qt): oc_ring.next() for h in range(4) for qt in range(8)}
                ores = {(h, qt): o_ring.next() for h in range(4) for qt in range(8)}

                def load_head(h):
                    (kT, B_kT), (v_sb, B_v), (strip, B_strip) = hres[h]
                    S.dma("sp", kT[0:64, 0, :], qk_d["B"][512 + h * 128:512 + h * 128 + 64, :], [B_qk["B"]], B_kT)
                    S.dma("sp", kT[64:128, 1, :], qk_d["B"][512 + h * 128 + 64:512 + h * 128 + 128, :], [B_qk["B"]], B_kT)
                    load_v(v_sb, B_v, vB_d[:, h * 128:(h + 1) * 128], 128)
                    S.dma("sp", strip, bt_d[8 + h, :, :], [B_bt], B_strip)

                def load_q(h, qt):
                    qs, B_q = qres[(h, qt)]
                    S.dma("sp", qs, qk_d["B"][h * 128:(h + 1) * 128, qt * 512:(qt + 1) * 512], [B_qk["B"]], B_q)

                def mk_tile(h, qt, c, kc, ps, pT, B_p, sb, B_sb):
                    (kT, B_kT), (v_sb, B_v), (strip, B_strip) = hres[h]
                    qs, B_q = qres[(h, qt)]
                    oc, B_oc = ocres[(h, qt)]
                    o_sb, B_o = ores[(h, qt)]

                    def sa():
                        if qt == 4 and c == 0 and kc == 0 and h < 3:
                            load_head(h + 1)
                        if c == 1 and kc == 0:
                            nq = (h, qt + 1) if qt < 7 else (h + 1, 0)
                            if nq[0] < 4:
                                load_q(*nq)
                        S.pe(lambda e: e.matmul(bk(ps), lhsT=kT[:, c, kc * 128:(kc + 1) * 128], rhs=qs, start=True, stop=True),
                             r=[B_kT, B_q], w=[PB[ps]])
                        d = kc * 128 - qt * 512
                        if -640 <= d <= 1024:
                            j0 = 1024 - d
                            S.dve(lambda e: e.tensor_tensor(out=sb, in0=bk(ps), in1=strip[:, j0:j0 + 512], op=ALU.add), r=[PB[ps], B_strip], w=[B_sb])
                            S.act(lambda e: e.activation(out=pT, in_=sb, func=AF.Exp), r=[B_sb], w=[B_p])
                        else:
                            ci = 2 * h + (0 if d > 0 else 1)
                            S.act(lambda e: e.activation(out=pT, in_=bk(ps), func=AF.Exp, bias=cfar_sb[:, ci:ci + 1]), r=[PB[ps], B_cfar], w=[B_p])

                    def sb_():
                        for qb in range(4):
                            S.pe(lambda e, qb=qb: e.matmul(bk(2 + qb)[:, 0:129], lhsT=pT[:, qb * 128:(qb + 1) * 128], rhs=v_sb[:, kc, :],
                                                           start=(kc == 0), stop=(kc == 31)), r=[B_p, B_v], w=[PB[2 + qb]])
                        if kc != 31:
                            return
                        raw, B_raw = raw_ring.next()
                        rd, B_rd = rd_ring.next()
                        for qb in range(4):
                            S.dve(lambda e, qb=qb: e.tensor_copy(out=raw[:, qb, :], in_=bk(2 + qb)[:, 0:129]), r=[PB[2 + qb]], w=[B_raw])
                        i0 = state["i"]

                        def norm():
                            S.dve(lambda e: e.reciprocal(out=rd, in_=raw[:, :, 128]), r=[B_raw], w=[B_rd])
                            for qb in range(4):
                                S.dve(lambda e, qb=qb: e.tensor_scalar(out=oc[:, c, qb, :], in0=raw[:, qb, 0:128], scalar1=rd[:, qb:qb + 1], scalar2=None, op0=ALU.mult),
                                      r=[B_raw, B_rd], w=[B_oc])
                        dq.append((i0 + 1, norm))
                        if c != 1:
                            return
                        sm, B_sm = sm2_ring.next()
                        cx = {}

                        def d2():
                            S.dve(lambda e: e.scalar_tensor_tensor(out=oc[:, 0], in0=oc[:, 1], scalar=neglam, in1=oc[:, 0], op0=ALU.mult, op1=ALU.add),
                                  r=[B_oc, B_lsc], w=[B_oc])
                            cx.update(epi.start(qt))

                        def d3():
                            for qb in range(4):
                                S.act(lambda e, qb=qb: e.activation(out=junk, in_=oc[:, 0, qb, :], func=AF.Square, accum_out=sm[:, qb:qb + 1]),
                                      r=[B_oc], w=[B_junk, B_sm])

                        def d4():
                            S.act(lambda e: e.activation(out=sm[:, 0:4], in_=sm[:, 0:4], func=AF.Ln, scale=1.0 / 128, bias=EPS), r=[B_sm], w=[B_sm])
                            S.act(lambda e: e.activation(out=sm[:, 4:8], in_=sm[:, 0:4], func=AF.Exp, scale=-0.5), r=[B_sm], w=[B_sm])

                        def d5():
                            for qb in range(4):
                                S.dve(lambda e, qb=qb: e.scalar_tensor_tensor(out=o_sb[:, qb, :], in0=oc[:, 0, qb, :], scalar=sm[:, 4 + qb:5 + qb], in1=subg,
                                                                              op0=ALU.mult, op1=ALU.mult), r=[B_oc, B_sm, B_subg], w=[B_o])
                            epi.mm(cx, 0)

                        def d7():
                            epi.fin(cx, 0, o_sb, B_o)
                            if qt == 7 and h < 3:
                                epi.load((h + 1) * 128)
                        dq.append((i0 + 2, d2))
                        dq.append((i0 + 3, d3))
                        dq.append((i0 + 4, d4))
                        dq.append((i0 + 5, d5))
                        dq.append((i0 + 8, d7))
                    return (sa, sb_)

                load_head(0)
                load_q(0, 0)
                epi.load(0)
                tiles = []
                ti = 0
                for h in range(4):
                    for qt in range(8):
                        for c in range(2):
                            for kc in range(32):
                                ps = SBK[ti % 3]
                                ti += 1
                                pT, B_p = p_ring.next()
                                sb, B_sb = sb_ring.next()
                                tiles.append(mk_tile(h, qt, c, kc, ps, pT, B_p, sb, B_sb))
                run_pipeline(tiles, LAG, dq, state)
                S.barrier()
                ar.pop()

            def phase_C():
                ar.push()
                kT_ring = Ring(ar, "kTc", 2, [2, SEQ], BF16)
                for (kT, B_kT) in kT_ring.items:
                    S.pool(lambda e, kT=kT: e.memset(kT[64:128, 0, :], 0.0), w=[B_kT])
                    S.pool(lambda e, kT=kT: e.memset(kT[0:64, 1, :], 0.0), w=[B_kT])
                v_ring = Ring(ar, "vc", 2, [32, 65], BF16)
                for (v_sb, B_v) in v_ring.items:
                    S.pool(lambda e, v_sb=v_sb: e.memset(v_sb[:, :, 64:65], 1.0), w=[B_v])
                q_ring = Ring(ar, "qc", 2, [2, 512], BF16)
                p_ring = Ring(ar, "pTc", 5, [512], BF16)
                o_ring = Ring(ar, "oc_", 2, [4, 256], F32)
                rd_ring = Ring(ar, "rdc", 3, [4], F32)
                raw_ring = Ring(ar, "rawc", 3, [4, 65], F32)
                dq = []
                state = {}
                epi = Epi(2, 256, 6)
                gres = [(kT_ring.next(), v_ring.next()) for g in range(2)]
                qres = {(g, qt): q_ring.next() for g in range(2) for qt in range(8)}
                ores = {(g, qt): o_ring.next() for g in range(2) for qt in range(8)}

                def load_group(g):
                    (kT, B_kT), (v_sb, B_v) = gres[g]
                    S.dma("sp", kT[0:64, 0, :], qk_d["C"][512 + g * 64:512 + (g + 1) * 64, :], [B_qk["C"]], B_kT)
                    S.dma("sp", kT[64:128, 1, :], qk_d["C"][512 + g * 64:512 + (g + 1) * 64, :], [B_qk["C"]], B_kT)
                    load_v(v_sb, B_v, vAC_d[:, 128 + g * 64:128 + (g + 1) * 64], 64)

                def load_q(g, qt):
                    qs, B_q = qres[(g, qt)]
                    S.dma("sp", qs, qk_d["C"][g * 256:(g + 1) * 256, qt * 512:(qt + 1) * 512].rearrange("(jj p) t -> p jj t", jj=2), [B_qk["C"]], B_q)

                def mk_tile(g, qt, j, kc, ps, pT, B_p):
                    (kT, B_kT), (v_sb, B_v) = gres[g]
                    qs, B_q = qres[(g, qt)]
                    o_sb, B_o = ores[(g, qt)]

                    def sa():
                        if g == 0 and qt == 4 and j == 0 and kc == 0:
                            load_group(1)
                        if j == 2 and kc == 0:
                            nq = (g, qt + 1) if qt < 7 else (g + 1, 0)
                            if nq[0] < 2:
                                load_q(*nq)
                        S.pe(lambda e: e.matmul(bk(ps), lhsT=kT[:, j % 2, kc * 128:(kc + 1) * 128], rhs=qs[:, j // 2, :], start=True, stop=True), r=[B_kT, B_q], w=[PB[ps]])
                        S.act(lambda e: e.activation(out=pT, in_=bk(ps), func=AF.Exp), r=[PB[ps]], w=[B_p])

                    def sb_():
                        for qb in range(4):
                            S.pe(lambda e, qb=qb: e.matmul(bk(2 + qb)[:, 0:65], lhsT=pT[:, qb * 128:(qb + 1) * 128], rhs=v_sb[:, kc, :],
                                                           start=(kc == 0), stop=(kc == 31)), r=[B_p, B_v], w=[PB[2 + qb]])
                        if kc != 31:
                            return
                        raw, B_raw = raw_ring.next()
                        rd, B_rd = rd_ring.next()
                        for qb in range(4):
                            S.dve(lambda e, qb=qb: e.tensor_copy(out=raw[:, qb, :], in_=bk(2 + qb)[:, 0:65]), r=[PB[2 + qb]], w=[B_raw])
                        i0 = state["i"]

                        def norm():
                            S.dve(lambda e: e.reciprocal(out=rd, in_=raw[:, :, 64]), r=[B_raw], w=[B_rd])
                            for qb in range(4):
                                S.dve(lambda e, qb=qb: e.tensor_scalar(out=o_sb[:, qb, j * 64:(j + 1) * 64], in0=raw[:, qb, 0:64], scalar1=rd[:, qb:qb + 1],
                                                                      scalar2=None, op0=ALU.mult), r=[B_raw, B_rd], w=[B_o])
                        dq.append((i0 + 1, norm))
                        if j != 3:
                            return
                        cx = {}

                        def d2():
                            cx.update(epi.start(qt))

                        def d3():
                            epi.mm(cx, 0)

                        def d6():
                            epi.fin(cx, 0, o_sb, B_o)
                            epi.mm(cx, 1)

                        def d9():
                            epi.fin(cx, 1, o_sb, B_o)
                            if qt == 7 and g == 0:
                                epi.load(256)
                        dq.append((i0 + 2, d2))
                        dq.append((i0 + 3, d3))
                        dq.append((i0 + 6, d6))
                        dq.append((i0 + 9, d9))
                    return (sa, sb_)

                load_group(0)
                load_q(0, 0)
                epi.load(0)
                tiles = []
                ti = 0
                for g in range(2):
                    for qt in range(8):
                        for j in range(4):
                            for kc in range(32):
                                ps = SBK[ti % 3]
                                ti += 1
                                pT, B_p = p_ring.next()
                                tiles.append(mk_tile(g, qt, j, kc, ps, pT, B_p))
                run_pipeline(tiles, LAG, dq, state)
                S.barrier()
                ar.pop()

            def phase_A():
                ar.push()
                kT_ring = Ring(ar, "kTa", 2, [2, SEQ], BF16)
                for (kT, B_kT) in kT_ring.items:
                    S.pool(lambda e, kT=kT: e.memset(kT[64:128, 0, :], 0.0), w=[B_kT])
                    S.pool(lambda e, kT=kT: e.memset(kT[0:64, 1, :], 0.0), w=[B_kT])
                v_ring = Ring(ar, "va", 2, [32, 65], BF16)
                for (v_sb, B_v) in v_ring.items:
                    S.pool(lambda e, v_sb=v_sb: e.memset(v_sb[:, :, 64:65], 1.0), w=[B_v])
                qa_ring = Ring(ar, "qa", 1, [2, SEQ], BF16)
                bA_ring = Ring(ar, "bA", 2, [4, 384], F32)
                sb_ring = Ring(ar, "sba", 3, [384], F32)
                p_ring = Ring(ar, "pTa", 5, [384], BF16)
                o_ring = Ring(ar, "oa", 2, [32, 256], F32)
                epi = Epi(0, 256, 6)
                dq = []
                state = {}
                sm_ring = Ring(ar, "sma", 4, [2], F32)
                gres = [(kT_ring.next(), v_ring.next(), qa_ring.next(), bA_ring.next(), o_ring.next()) for g in range(2)]

                def load_group(g, with_q=True):
                    (kT, B_kT), (v_sb, B_v), (qa, B_qa), (bA, B_bA), _ = gres[g]
                    S.dma("sp", kT[0:64, 0, :], qk_d["A"][512 + g * 64:512 + (g + 1) * 64, :], [B_qk["A"]], B_kT)
                    S.dma("sp", kT[64:128, 1, :], qk_d["A"][512 + g * 64:512 + (g + 1) * 64, :], [B_qk["A"]], B_kT)
                    load_v(v_sb, B_v, vAC_d[:, g * 64:(g + 1) * 64], 64)
                    S.dma("sp", bA, bt_d[g * 4:(g + 1) * 4, :, 0:384].rearrange("h p w -> p h w"), [B_bt], B_bA)

                def load_qa(g):
                    (qa, B_qa) = gres[g][2]
                    S.dma("sp", qa, qk_d["A"][g * 256:(g + 1) * 256, :].rearrange("(jj p) t -> p jj t", jj=2), [B_qk["A"]], B_qa)

                def mk_tile(g, j, kc, ps, pT, B_p, sb, B_sb):
                    (kT, B_kT), (v_sb, B_v), (qa, B_qa), (bA, B_bA), (o_sb, B_o) = gres[g]
                    hh = g * 4 + j
                    ulo, uhi = max(kc - 1, 0), min(kc + 1, 31)
                    w_ = 128 * (uhi - ulo + 1)
                    boff = (ulo - (kc - 1)) * 128

                    def sa():
                        if g == 0 and j == 2 and kc == 0:
                            load_group(1)
                        if g == 1 and j == 0 and kc == 0:
                            load_qa(1)
                        S.pe(lambda e: e.matmul(bk(ps)[:, 0:w_], lhsT=kT[:, j % 2, kc * 128:(kc + 1) * 128], rhs=qa[:, j // 2, ulo * 128:ulo * 128 + w_], start=True, stop=True),
                             r=[B_kT, B_qa], w=[PB[ps]])
                        S.dve(lambda e: e.tensor_tensor(out=sb[:, 0:w_], in0=bk(ps)[:, 0:w_], in1=bA[:, j, boff:boff + w_], op=ALU.add), r=[PB[ps], B_bA], w=[B_sb])
                        S.act(lambda e: e.activation(out=pT[:, 0:w_], in_=sb[:, 0:w_], func=AF.Exp), r=[B_sb], w=[B_p])

                    def sb_():
                        for u in range(ulo, uhi + 1):
                            ab = 2 + (u % 4)
                            S.pe(lambda e, ab=ab, u=u: e.matmul(bk(ab)[:, 0:65], lhsT=pT[:, (u - ulo) * 128:(u - ulo + 1) * 128], rhs=v_sb[:, kc, :],
                                                                start=(kc == max(u - 1, 0)), stop=(kc == min(u + 1, 31))), r=[B_p, B_v], w=[PB[ab]])
                            if kc == min(u + 1, 31):
                                sm, B_sm = sm_ring.next()
                                S.dve(lambda e, sm=sm, ab=ab: e.tensor_tensor(out=sm[:, 0:1], in0=bk(ab)[:, 64:65], in1=esink[:, hh:hh + 1], op=ALU.add),
                                      r=[PB[ab], B_esink], w=[B_sm])
                                S.dve(lambda e, sm=sm: e.reciprocal(out=sm[:, 1:2], in_=sm[:, 0:1]), r=[B_sm], w=[B_sm])
                                S.act(lambda e, sm=sm, ab=ab, u=u: e.activation(out=o_sb[:, u, j * 64:(j + 1) * 64], in_=bk(ab)[:, 0:64], func=AF.Copy, scale=sm[:, 1:2]),
                                      r=[PB[ab], B_sm], w=[B_o])
                        if j == 3 and kc == 31:
                            i0 = state["i"]
                            for qt in range(8):
                                def er(qt=qt):
                                    epi.run(qt, o_sb[:, qt * 4:(qt + 1) * 4, :], B_o)
                                    if qt == 7 and g == 0:
                                        epi.load(256)
                                dq.append((i0 + 2 + 3 * qt, er))
                    return (sa, sb_)

                load_group(0)
                load_qa(0)
                epi.load(0)
                tiles = []
                ti = 0
                for g in range(2):
                    for j in range(4):
                        for kc in range(32):
                            ps = SBK[ti % 3]
                            ti += 1
                            pT, B_p = p_ring.next()
                            sb, B_sb = sb_ring.next()
                            tiles.append(mk_tile(g, j, kc, ps, pT, B_p, sb, B_sb))
                run_pipeline(tiles, LAG, dq, state)
                S.barrier()
                ar.pop()

            def phase_D():
                ar.push()
                HB = {1: 2, 4: 8, 16: 8}
                bD_ring = Ring(ar, "bD", 2, [8, 256], F32)
                sb_ring = Ring(ar, "sbd", 3, [256], F32)
                p_ring = Ring(ar, "pTd", 5, [256], BF16)
                stg_ring = Ring(ar, "stg", 3, [33, 65], F32)
                ti = 0
                ai = 0
                for g3 in range(3):
                    r = D_R[g3]
                    M = SEQ // r
                    nj = M // 128
                    hb_n = HB[r]
                    bD, B_bD = bD_ring.next()
                    S.dma("sp", bD, bt_d[12 + 8 * g3:20 + 8 * g3, :, 0:256].rearrange("h p w -> p h w"), [B_bt], B_bD)
                    ar.push()
                    kT_ring = Ring(ar, "kTd", 2, [hb_n, M], BF16)
                    q_ring = Ring(ar, "qd", 2, [hb_n // 2, M + 128], BF16)
                    v_ring = Ring(ar, "vd", 2, [hb_n, nj, 65], BF16)
                    for (v_sb, B_v) in v_ring.items:
                        S.pool(lambda e, v_sb=v_sb: e.memset(v_sb[:, :, :, 64:65], 1.0), w=[B_v])
                    for (q_sb, B_q) in q_ring.items:
                        S.pool(lambda e, q_sb=q_sb: e.memset(q_sb[:, :, 0:64], 0.0), w=[B_q])
                        S.pool(lambda e, q_sb=q_sb, M=M: e.memset(q_sb[:, :, 64 + M:128 + M], 0.0), w=[B_q])
                    for (kT, B_kT) in kT_ring.items:
                        S.pool(lambda e, kT=kT: e.memset(kT[64:128, 0::2, :], 0.0), w=[B_kT])
                        S.pool(lambda e, kT=kT: e.memset(kT[0:64, 1::2, :], 0.0), w=[B_kT])
                    batches = [(cc, h0) for cc in range(r) for h0 in range(0, 8, hb_n)]
                    bres = [(kT_ring.next(), q_ring.next(), v_ring.next()) for _ in batches]

                    def load_batch(bi, g3=g3, M=M, hb_n=hb_n, batches=batches, bres=bres):
                        cc, h0 = batches[bi]
                        (kT, B_kT), (q_sb, B_q), (v_sb, B_v) = bres[bi]
                        krow = 1536 + g3 * 512 + h0 * 64
                        qrow = g3 * 512 + h0 * 64
                        ksrc = qk_d["D"][krow:krow + hb_n * 64, cc * M:(cc + 1) * M].rearrange("(hp two d) t -> two d hp t", two=2, d=64)
                        S.dma("sp", kT[0:64, 0::2, :], ksrc[0], [B_qk["D"]], B_kT)
                        S.dma("sp", kT[64:128, 1::2, :], ksrc[1], [B_qk["D"]], B_kT)
                        S.dma("sp", q_sb[:, :, 64:64 + M], qk_d["D"][qrow:qrow + hb_n * 64, cc * M:(cc + 1) * M].rearrange("(hp p) t -> p hp t", p=128), [B_qk["D"]], B_q)
                        for hi in range(hb_n):
                            S.dma("sp", v_sb[:, hi, :, 0:64], vD_d[g3, cc * M:(cc + 1) * M, (h0 + hi) * 64:(h0 + hi + 1) * 64].rearrange("(j p) e -> p j e", p=128), [B_vD], B_v)

                    def mk_tile(bi, hi, kc, ps, pT, B_p, sb, B_sb, stg, B_stg, a_prev, a_new, g3=g3, r=r, M=M, nj=nj, hb_n=hb_n, batches=batches, bres=bres, bD=bD, B_bD=B_bD):
                        cc, h0 = batches[bi]
                        (kT, B_kT), (q_sb, B_q), (v_sb, B_v) = bres[bi]
                        hs = h0 + hi

                        def sa():
                            if hi * nj + kc == LAG and bi + 1 < len(batches):
                                load_batch(bi + 1)
                            S.pe(lambda e: e.matmul(bk(ps)[:, 0:256], lhsT=kT[:, hi, kc * 128:(kc + 1) * 128], rhs=q_sb[:, hi // 2, kc * 128:kc * 128 + 256], start=True, stop=True),
                                 r=[B_kT, B_q], w=[PB[ps]])
                            S.dve(lambda e: e.tensor_tensor(out=sb, in0=bk(ps)[:, 0:256], in1=bD[:, hs, :], op=ALU.add), r=[PB[ps], B_bD], w=[B_sb])
                            S.act(lambda e: e.activation(out=pT, in_=sb, func=AF.Exp), r=[B_sb], w=[B_p])

                        def sb_():
                            S.pe(lambda e: e.matmul(bk(a_prev)[:, 0:65], lhsT=pT[:, 0:128], rhs=v_sb[:, hi, kc, :], start=(kc == 0), stop=True), r=[B_p, B_v], w=[PB[a_prev]])
                            S.act(lambda e: e.copy(out=stg[:, kc, :], in_=bk(a_prev)[:, 0:65]), r=[PB[a_prev]], w=[B_stg])
                            S.pe(lambda e: e.matmul(bk(a_new)[:, 0:65], lhsT=pT[:, 128:256], rhs=v_sb[:, hi, kc, :], start=True, stop=(kc == nj - 1)), r=[B_p, B_v], w=[PB[a_new]])
                            if kc != nj - 1:
                                return
                            S.act(lambda e: e.copy(out=stg[:, nj, :], in_=bk(a_new)[:, 0:65]), r=[PB[a_new]], w=[B_stg])
                            base = g3 * SEQ * 520 + hs * 65
                            dst = bass.AP(dacc_t, base + cc * 520, [[r * 520, 64], [1, 65]])
                            S.dma("pool", dst, stg[64:128, 0, :], [B_stg], B_dacc)
                            if nj > 1:
                                dst = bass.AP(dacc_t, base + (cc + r * 64) * 520, [[r * 520, 128], [128 * r * 520, nj - 1], [1, 65]])
                                S.dma("pool", dst, stg[:, 1:nj, :], [B_stg], B_dacc)
                            dst = bass.AP(dacc_t, base + (cc + r * (M - 64)) * 520, [[r * 520, 64], [1, 65]])
                            S.dma("pool", dst, stg[0:64, nj, :], [B_stg], B_dacc)
                        return (sa, sb_)

                    load_batch(0)
                    tiles = []
                    for bi in range(len(batches)):
                        for hi in range(hb_n):
                            stg, B_stg = stg_ring.next()
                            cur = None
                            for kc in range(nj):
                                ps = SBK[ti % 3]
                                ti += 1
                                pT, B_p = p_ring.next()
                                sb, B_sb = sb_ring.next()
                                if kc == 0:
                                    cur = 2 + (ai % 4)
                                    ai += 1
                                a_prev = cur
                                cur = 2 + (ai % 4)
                                ai += 1
                                a_new = cur
                                tiles.append(mk_tile(bi, hi, kc, ps, pT, B_p, sb, B_sb, stg, B_stg, a_prev, a_new))
                    run_pipeline(tiles, LAG)
                    S.barrier()
                    ar.pop()
                ar.push()
                da_ring = Ring(ar, "da", 2, [3, 8, 65], F32)
                o_ring = Ring(ar, "od", 2, [4, 512], F32)
                sm_ring = Ring(ar, "smd", 2, [8], F32)
                epi = Epi(3, 512, 6)
                epi.load(0)
                for qt in range(8):
                    o_sb, B_o = o_ring.next()
                    for qb in range(4):
                        tt = qt * 4 + qb
                        da, B_da = da_ring.next()
                        sm, B_sm = sm_ring.next()
                        S.dma("sp", da, dacc_d[:, tt * 128:(tt + 1) * 128, :].rearrange("g p (h e) -> p g h e", h=8), [B_dacc], B_da)
                        S.dve(lambda e, da=da: e.tensor_tensor(out=da[:, 0], in0=da[:, 0], in1=da[:, 1], op=ALU.add), r=[B_da], w=[B_da])
                        S.dve(lambda e, da=da: e.tensor_tensor(out=da[:, 0], in0=da[:, 0], in1=da[:, 2], op=ALU.add), r=[B_da], w=[B_da])
                        S.dve(lambda e, da=da, sm=sm: e.reciprocal(out=sm, in_=da[:, 0, :, 64]), r=[B_da], w=[B_sm])
                        for hs in range(8):
                            S.dve(lambda e, da=da, sm=sm, hs=hs, o_sb=o_sb, qb=qb: e.tensor_scalar(out=o_sb[:, qb, hs * 64:(hs + 1) * 64], in0=da[:, 0, hs, 0:64],
                                                                                                 scalar1=sm[:, hs:hs + 1], scalar2=None, op0=ALU.mult), r=[B_da, B_sm], w=[B_o])
                    epi.run(qt, o_sb, B_o)
                S.barrier()
                ar.pop()
                ar.pop()

            def phase_F():
                ar.push()
                wbr = ar.alloc([4, 4, DM], BF16)
                wmg = ar.alloc([8, 4096], BF16)
                wo = ar.alloc([8, DM], BF16)
                B_wbr, B_wmg, B_wo = Buf("wbr"), Buf("wmg"), Buf("wo")
                for n in range(4):
                    S.dma("pool", wbr[:, n], w_br[l, n].rearrange("(kc p) o -> p kc o", p=128), [B_const], B_wbr, heavy=True)
                for n in range(4):
                    S.dma("pool", wmg[:, :, n * 1024:(n + 1) * 1024], wl[:, COL["merge"] + n * 1024:COL["merge"] + (n + 1) * 1024].rearrange("(kc p) c -> p kc c", p=128), [B_const], B_wmg, heavy=True)
                S.dma("pool", wo, w_out[l].rearrange("(kc p) o -> p kc o", p=128), [B_const], B_wo, heavy=True)
                g_ring = Ring(ar, "gin", 2, [2048], BF16)
                gT = ar.alloc([16, 512], BF16)
                B_gT = Buf("gT")
                hq_ring = Ring(ar, "hqf", 1, [8, 512], BF16)
                mT = ar.alloc([8, 512], F32)
                B_mT = Buf("mT")
                mTb = ar.alloc([8, 512], BF16)
                B_mTb = Buf("mTb")
                sg_ring = Ring(ar, "sgf", 2, [512], F32)
                xq_ring = Ring(ar, "xq", 2, [DM], F32)
                ti = 0
                for qt in range(8):
                    hq, B_hq = hq_ring.next()
                    S.dma("sp", hq, hT_d[:, qt * 512:(qt + 1) * 512].rearrange("(kc p) t -> p kc t", p=128), [B_hT], B_hq)
                    for qb in range(4):
                        gin, B_gin = g_ring.next()
                        tt = qt * 4 + qb
                        S.dma("sp", gin, gated_d[tt * 128:(tt + 1) * 128, :], [B_gated], B_gin)
                        for half in range(2):
                            pb = ti % 2
                            ti += 1
                            tp = bk(pb).bitcast(BF16)
                            for k8 in range(8):
                                kcg = half * 8 + k8
                                S.pe(lambda e, tp=tp, gin=gin, k8=k8, kcg=kcg: e.transpose(out=tp[:, k8 * 128:(k8 + 1) * 128], in_=gin[:, kcg * 128:(kcg + 1) * 128], identity=ident_bf),
                                     r=[B_gin, Bk_const], w=[PB[pb]])
                            S.dve(lambda e, tp=tp, half=half, qb=qb: e.tensor_copy(out=gT[:, half * 8:(half + 1) * 8, qb * 128:(qb + 1) * 128], in_=tp.rearrange("p (a b) -> p a b", a=8)),
                                  r=[PB[pb]], w=[B_gT])
                    yi = 0
                    for n in range(4):
                        for oc in range(8):
                            py = 2 + (yi % 2)
                            pm = 4 + (yi % 2)
                            yi += 1
                            for kc in range(4):
                                S.pe(lambda e, py=py, n=n, kc=kc, oc=oc: e.matmul(bk(py), lhsT=wbr[:, n, kc, oc * 128:(oc + 1) * 128], rhs=gT[:, n * 4 + kc, :], start=(kc == 0), stop=(kc == 3)),
                                     r=[B_wbr, B_gT], w=[PB[py]])
                            for kc in range(8):
                                S.pe(lambda e, pm=pm, n=n, kc=kc, oc=oc, hq=hq: e.matmul(bk(pm), lhsT=wmg[:, kc, n * 1024 + oc * 128:n * 1024 + (oc + 1) * 128], rhs=hq[:, kc, :],
                                                                                      start=(kc == 0), stop=(kc == 7)), r=[B_wmg, B_hq], w=[PB[pm]])
                            sg, B_sg = sg_ring.next()
                            S.act(lambda e, sg=sg, pm=pm: e.activation(out=sg, in_=bk(pm), func=AF.Sigmoid), r=[PB[pm]], w=[B_sg])
                            if n == 0:
                                S.dve(lambda e, sg=sg, py=py, oc=oc: e.tensor_tensor(out=mT[:, oc, :], in0=sg, in1=bk(py), op=ALU.mult), r=[B_sg, PB[py]], w=[B_mT])
                            else:
                                S.dve(lambda e, sg=sg, py=py: e.tensor_tensor(out=sg, in0=sg, in1=bk(py), op=ALU.mult), r=[B_sg, PB[py]], w=[B_sg])
                                if n < 3:
                                    S.pool(lambda e, sg=sg, oc=oc: e.tensor_tensor(out=mT[:, oc, :], in0=mT[:, oc, :], in1=sg, op=ALU.add), r=[B_sg, B_mT], w=[B_mT])
                                else:
                                    S.pool(lambda e, sg=sg, oc=oc: e.tensor_tensor(out=mTb[:, oc, :], in0=mT[:, oc, :], in1=sg, op=ALU.add), r=[B_sg, B_mT], w=[B_mTb])
                    for qb in range(4):
                        tt = qt * 4 + qb
                        xq, B_xq = xq_ring.next()
                        S.dma("sp", xq, x_src[tt * 128:(tt + 1) * 128, :], [B_xs], B_xq)
                        for half in range(2):
                            po = 6 + (half % 2)
                            for kc in range(8):
                                S.pe(lambda e, po=po, kc=kc, qb=qb, half=half: e.matmul(bk(po), lhsT=mTb[:, kc, qb * 128:(qb + 1) * 128], rhs=wo[:, kc, half * 512:(half + 1) * 512],
                                                                                     start=(kc == 0), stop=(kc == 7)), r=[B_mTb, B_wo], w=[PB[po]])
                            S.dve(lambda e, po=po, xq=xq, half=half: e.tensor_tensor(out=xq[:, half * 512:(half + 1) * 512], in0=xq[:, half * 512:(half + 1) * 512], in1=bk(po), op=ALU.add),
                                  r=[PB[po], B_xq], w=[B_xq])
                        S.dma("pool", x_dst[tt * 128:(tt + 1) * 128, :], xq, [B_xq], B_xd)
                S.barrier()
                ar.pop()

            if "P" in phases:
                phase_P()
            if "A" in phases:
                phase_A()
            if "B" in phases:
                phase_B()
            if "C" in phases:
                phase_C()
            if "D" in phases:
                phase_D()
            if "F" in phases:
                phase_F()
            ar.pop()

        S.finish([B_out, B_xmid, B_hT, B_gated, B_dacc, B_bt, B_vec, B_vAC, B_vB, B_vD] + list(B_qk.values()))
        S.finalize()
        with nc.Block() as block:
            @block.sync
            def _(e):
                S.emit("sp", e)

            @block.scalar
            def _(e):
                S.emit("act", e)

            @block.gpsimd
            def _(e):
                S.emit("pool", e)

            @block.tensor
            def _(e):
                S.emit("pe", e)

            @block.vector
            def _(e):
                S.emit("dve", e)
    return nc


_CONSTS = None


def kernel(x, w_in, w_branch, w_out, norm_gain, qk_gain, sink, lambda_vec, sub_norm_gain, rel_bias):
    global _CONSTS
    if _CONSTS is None:
        _CONSTS = make_consts()
    f = lambda a: np.ascontiguousarray(np.asarray(a, dtype=np.float32))
    shared = dict(w_in=f(w_in), w_branch=f(w_branch), w_out=f(w_out), norm_gain=f(norm_gain), qk_gain=f(qk_gain), sink=f(sink),
                  lambda_vec=f(lambda_vec), sub_norm_gain=f(sub_norm_gain), rel_bias=f(rel_bias), **_CONSTS)
    x = f(x)
    nc = build_program()
    in_maps = [dict(x=x[i], **shared) for i in range(8)]
    res = run_bass_kernel_spmd(nc, in_maps, core_ids=list(range(8)))
    return np.stack([np.asarray(r["out"], dtype=np.float32) for r in res.results], axis=0)
```
